# Optimizing a Trainium2 kernel written in Bass

```python
import jax, jax.numpy as jnp
from jax import lax
import numpy as np

D_MODEL = 1024
BATCH = 8
SEQ = 2048
DEPTH = 4
DEC_BATCH = 8
DEC_SEQ = 16
PAST_LEN = 2048

CHUNK = 64
HEAD_DIM = 64
EPS = 1e-6
A_HEADS = 8
A_WIDTH = A_HEADS * HEAD_DIM
SB_BLOCK = 128
B_HEADS = 4
B_DK = 64
B_DV = 128
B_KW = B_HEADS * B_DK
B_VW = B_HEADS * B_DV
B_GATE_RANK = 16
B_GATE_TEMP = 16.0
GLA_CHUNK = 64
C_HEADS = 16
C_WIDTH = C_HEADS * HEAD_DIM
C_LEFT_CHUNKS = 8
C_WINDOW = C_LEFT_CHUNKS * CHUNK
REL_MIN = -(CHUNK - 1)
REL_MAX = 128
N_REL = REL_MAX - REL_MIN + 1
D_FF = -(-8 * D_MODEL // (3 * 256)) * 256
N_AB = (DEPTH + 1) // 2
N_C = DEPTH // 2
AB_IN = 3 * A_WIDTH + 2 * B_KW + B_VW + B_GATE_RANK + B_VW
AB_OUT = A_WIDTH + B_VW

kernel_name = "hybrid_stickbreak_gla_chunkband_stream_step"


def _rmsnorm(x, g):
    xf = x.astype(jnp.float32)
    y = xf * lax.rsqrt(jnp.mean(xf * xf, axis=-1, keepdims=True) + EPS)
    return (y * g.astype(jnp.float32)).astype(x.dtype)


def _split_cols(z, sizes):
    out, start = [], 0
    for s in sizes:
        out.append(z[..., start:start + s])
        start += s
    return out


def _swiglu(h, wg, wu, wd):
    return (jax.nn.silu(h @ wg) * (h @ wu)) @ wd


def _sb_block(q, k, v, q_pos, k_pos):
    z = jnp.einsum('bthd,bshd->bhts', q, k).astype(jnp.float32) * (HEAD_DIM ** -0.5)
    mask = k_pos[None, :] < q_pos[:, None]
    log_beta = jax.nn.log_sigmoid(z)
    log_keep = jnp.where(mask, jax.nn.log_sigmoid(-z), 0.0)
    rev = lax.cumsum(log_keep, axis=3, reverse=True)
    after = jnp.concatenate([rev[..., 1:], jnp.zeros_like(rev[..., :1])], axis=-1)
    w = jnp.where(mask, jnp.exp(log_beta + after), 0.0)
    return jnp.einsum('bhts,bshd->bthd', w.astype(v.dtype), v)


def _sb_prompt(q, k, v):
    B, S, H, d = q.shape
    nb = S // SB_BLOCK
    qb = jnp.moveaxis(q.reshape(B, nb, SB_BLOCK, H, d), 1, 0)
    pos = jnp.arange(S, dtype=jnp.int32)
    pb = pos.reshape(nb, SB_BLOCK)
    out = lax.map(lambda a: _sb_block(a[0], k, v, a[1], pos), (qb, pb))
    return jnp.moveaxis(out, 0, 1).reshape(B, S, H, d)


def _gla(q, k, v, log_a, s0, L):
    B, T, H, dk = q.shape
    n = T // L

    def chunks(t):
        return jnp.moveaxis(t.astype(jnp.float32).reshape(B, n, L, H, t.shape[-1]), 1, 0)

    qc = chunks(q) * (dk ** -0.5)
    kc, vc, gc = chunks(k), chunks(v), chunks(log_a)
    causal = jnp.tril(jnp.ones((L, L), dtype=bool))

    def step(S, inp):
        qi, ki, vi, gi = inp
        b = jnp.cumsum(gi, axis=1)
        b_last = b[:, -1]
        qg = qi * jnp.exp(b)
        kg = ki * jnp.exp(-b)
        att = jnp.where(causal, jnp.einsum('bthd,bshd->bhts', qg, kg), 0.0)
        o = jnp.einsum('bhts,bshv->bthv', att, vi) + jnp.einsum('bthd,bhdv->bthv', qg, S)
        kd = ki * jnp.exp(b_last[:, None] - b)
        S = jnp.exp(b_last)[..., None] * S + jnp.einsum('bshd,bshv->bhdv', kd, vi)
        return S, o

    S, o = lax.scan(step, s0.astype(jnp.float32), (qc, kc, vc, gc))
    return jnp.moveaxis(o, 0, 1).reshape(B, T, H, -1), S


def _ab_project(h, w_in, w_gate, b_gate):
    B, T, _ = h.shape
    z = h @ w_in
    qa, ka, va, qb, kb, vb, g_lr, r = _split_cols(
        z, (A_WIDTH, A_WIDTH, A_WIDTH, B_KW, B_KW, B_VW, B_GATE_RANK, B_VW))
    log_a = jax.nn.log_sigmoid((g_lr @ w_gate + b_gate).astype(jnp.float32)) / B_GATE_TEMP
    rs = lambda t, H: t.reshape(B, T, H, -1)
    return (rs(qa, A_HEADS), rs(ka, A_HEADS), rs(va, A_HEADS),
            rs(qb, B_HEADS), rs(kb, B_HEADS), rs(vb, B_HEADS), rs(log_a, B_HEADS), r)


def _ab_merge(o_a, o_b, r, g_gla, w_out):
    B, T = o_a.shape[:2]
    ob = o_b.astype(jnp.float32)
    ob = ob * lax.rsqrt(jnp.mean(ob * ob, axis=-1, keepdims=True) + EPS)
    ob = ob.reshape(B, T, B_VW) * g_gla.astype(jnp.float32) * jax.nn.silu(r.astype(jnp.float32))
    cat = jnp.concatenate([o_a.reshape(B, T, A_WIDTH).astype(jnp.float32), ob], axis=-1)
    return cat @ w_out


def _band_attn(q, k, v, q_pos, k_pos, rel_table):
    s = jnp.einsum('bthd,bshd->bhts', q, k).astype(jnp.float32) * (HEAD_DIM ** -0.5)
    qc = q_pos // CHUNK
    kc = k_pos // CHUNK
    mask = ((kc[None, :] <= qc[:, None]) & (kc[None, :] >= qc[:, None] - C_LEFT_CHUNKS)
            & (k_pos[None, :] >= 0))
    rel = jnp.clip(q_pos[:, None] - k_pos[None, :], REL_MIN, REL_MAX) - REL_MIN
    s = s + rel_table[:, rel].astype(jnp.float32)[None]
    p = jax.nn.softmax(jnp.where(mask, s, -jnp.inf), axis=-1)
    return jnp.einsum('bhts,bshd->bthd', p.astype(v.dtype), v)


def _band_prompt(q, k, v, rel_table):
    B, S, H, d = q.shape
    n = S // CHUNK
    band = C_WINDOW + CHUNK
    pad = ((0, 0), (C_WINDOW, 0), (0, 0), (0, 0))
    kp, vp = jnp.pad(k, pad), jnp.pad(v, pad)
    qb = jnp.moveaxis(q.reshape(B, n, CHUNK, H, d), 1, 0)

    def one(args):
        qi, c = args
        start = c * CHUNK
        ki = lax.dynamic_slice_in_dim(kp, start, band, axis=1)
        vi = lax.dynamic_slice_in_dim(vp, start, band, axis=1)
        q_pos = start + jnp.arange(CHUNK, dtype=jnp.int32)
        k_pos = start - C_WINDOW + jnp.arange(band, dtype=jnp.int32)
        return _band_attn(qi, ki, vi, q_pos, k_pos, rel_table)

    out = lax.map(one, (qb, jnp.arange(n, dtype=jnp.int32)))
    return jnp.moveaxis(out, 0, 1).reshape(B, S, H, d)


def setup_inputs(seed: int = 0) -> dict:
    key = jax.random.key(seed)
    ks = jax.random.split(key, 21)

    def nrm(k, shape, scale):
        return jax.random.normal(k, shape, jnp.float32) * scale

    c_rows = min(C_WINDOW, PAST_LEN)
    return {
        "x_prompt": nrm(ks[0], (BATCH, SEQ, D_MODEL), 1.0),
        "x_sample": nrm(ks[1], (DEC_BATCH, DEC_SEQ, D_MODEL), 1.0),
        "cache_a_k": nrm(ks[2], (N_AB, DEC_BATCH, PAST_LEN, A_HEADS, HEAD_DIM), 1.0),
        "cache_a_v": nrm(ks[3], (N_AB, DEC_BATCH, PAST_LEN, A_HEADS, HEAD_DIM), 1.0),
        "state_b": nrm(ks[4], (N_AB, DEC_BATCH, B_HEADS, B_DK, B_DV), 0.1),
        "cache_c_k": nrm(ks[5], (N_C, DEC_BATCH, c_rows, C_HEADS, HEAD_DIM), 1.0),
        "cache_c_v": nrm(ks[6], (N_C, DEC_BATCH, c_rows, C_HEADS, HEAD_DIM), 1.0),
        "norm_mix_g": 1.0 + nrm(ks[7], (DEPTH, D_MODEL), 0.01),
        "norm_ffn_g": 1.0 + nrm(ks[8], (DEPTH, D_MODEL), 0.01),
        "w_in_ab": nrm(ks[9], (N_AB, D_MODEL, AB_IN), D_MODEL ** -0.5),
        "w_gate_b": nrm(ks[10], (N_AB, B_GATE_RANK, B_KW), B_GATE_RANK ** -0.5),
        "b_gate_b": nrm(ks[11], (N_AB, B_KW), 0.1),
        "norm_gla_g": 1.0 + nrm(ks[12], (N_AB, B_VW), 0.01),
        "w_out_ab": nrm(ks[13], (N_AB, AB_OUT, D_MODEL), AB_OUT ** -0.5),
        "w_qkv_c": nrm(ks[14], (N_C, D_MODEL, 3 * C_WIDTH), D_MODEL ** -0.5),
        "rel_bias_c": nrm(ks[15], (N_C, C_HEADS, N_REL), 0.1),
        "w_out_c": nrm(ks[16], (N_C, C_WIDTH, D_MODEL), C_WIDTH ** -0.5),
        "w_ffn_gate": nrm(ks[17], (DEPTH, D_MODEL, D_FF), D_MODEL ** -0.5),
        "w_ffn_up": nrm(ks[18], (DEPTH, D_MODEL, D_FF), D_MODEL ** -0.5),
        "w_ffn_down": nrm(ks[19], (DEPTH, D_FF, D_MODEL), D_FF ** -0.5),
        "norm_final_g": 1.0 + nrm(ks[20], (D_MODEL,), 0.01),
    }


def reference(x_prompt, x_sample, cache_a_k, cache_a_v, state_b, cache_c_k, cache_c_v,
              norm_mix_g, norm_ffn_g, w_in_ab, w_gate_b, b_gate_b, norm_gla_g, w_out_ab,
              w_qkv_c, rel_bias_c, w_out_c, w_ffn_gate, w_ffn_up, w_ffn_down, norm_final_g):
    xp, xs = x_prompt, x_sample
    Bp, Tp, _ = xp.shape
    Bs, Ts, _ = xs.shape
    P = PAST_LEN
    q_pos_s = P + jnp.arange(Ts, dtype=jnp.int32)
    a_kp, a_vp, a_ks, a_vs, b_sp, b_ss = [], [], [], [], [], []
    c_kp, c_vp, c_ks, c_vs = [], [], [], []

    for layer in range(DEPTH):
        i = layer // 2
        hp = _rmsnorm(xp, norm_mix_g[layer])
        hs = _rmsnorm(xs, norm_mix_g[layer])
        if layer % 2 == 0:
            qa, ka, va, qb, kb, vb, la, r = _ab_project(hp, w_in_ab[i], w_gate_b[i], b_gate_b[i])
            oa = _sb_prompt(qa, ka, va)
            s0 = jnp.zeros((Bp, B_HEADS, B_DK, B_DV), jnp.float32)
            ob, sbp = _gla(qb, kb, vb, la, s0, GLA_CHUNK)
            xp = xp + _ab_merge(oa, ob, r, norm_gla_g[i], w_out_ab[i]).astype(xp.dtype)

            qa2, ka2, va2, qb2, kb2, vb2, la2, r2 = _ab_project(hs, w_in_ab[i], w_gate_b[i], b_gate_b[i])
            k_all = jnp.concatenate([cache_a_k[i].astype(ka2.dtype), ka2], axis=1)
            v_all = jnp.concatenate([cache_a_v[i].astype(va2.dtype), va2], axis=1)
            oa2 = _sb_block(qa2, k_all, v_all, q_pos_s, jnp.arange(P + Ts, dtype=jnp.int32))
            ob2, sbs = _gla(qb2, kb2, vb2, la2, state_b[i], Ts)
            xs = xs + _ab_merge(oa2, ob2, r2, norm_gla_g[i], w_out_ab[i]).astype(xs.dtype)

            a_kp.append(ka); a_vp.append(va); a_ks.append(ka2); a_vs.append(va2)
            b_sp.append(sbp); b_ss.append(sbs)
        else:
            qc, kc, vc = [t.reshape(Bp, Tp, C_HEADS, HEAD_DIM)
                          for t in _split_cols(hp @ w_qkv_c[i], (C_WIDTH, C_WIDTH, C_WIDTH))]
            oc = _band_prompt(qc, kc, vc, rel_bias_c[i])
            xp = xp + (oc.reshape(Bp, Tp, C_WIDTH) @ w_out_c[i]).astype(xp.dtype)

            qc2, kc2, vc2 = [t.reshape(Bs, Ts, C_HEADS, HEAD_DIM)
                             for t in _split_cols(hs @ w_qkv_c[i], (C_WIDTH, C_WIDTH, C_WIDTH))]
            Wc = cache_c_k.shape[2]
            k_all = jnp.concatenate([cache_c_k[i].astype(kc2.dtype), kc2], axis=1)
            v_all = jnp.concatenate([cache_c_v[i].astype(vc2.dtype), vc2], axis=1)
            k_pos = P - Wc + jnp.arange(Wc + Ts, dtype=jnp.int32)
            oc2 = _band_attn(qc2, k_all, v_all, q_pos_s, k_pos, rel_bias_c[i])
            xs = xs + (oc2.reshape(Bs, Ts, C_WIDTH) @ w_out_c[i]).astype(xs.dtype)

            keep = min(C_WINDOW, Tp)
            c_kp.append(kc[:, Tp - keep:]); c_vp.append(vc[:, Tp - keep:])
            c_ks.append(kc2); c_vs.append(vc2)
        xp = xp + _swiglu(_rmsnorm(xp, norm_ffn_g[layer]), w_ffn_gate[layer], w_ffn_up[layer],
                          w_ffn_down[layer]).astype(xp.dtype)
        xs = xs + _swiglu(_rmsnorm(xs, norm_ffn_g[layer]), w_ffn_gate[layer], w_ffn_up[layer],
                          w_ffn_down[layer]).astype(xs.dtype)

    y_prompt = _rmsnorm(xp, norm_final_g)
    y_sample = _rmsnorm(xs, norm_final_g)
    a_k_prompt = jnp.stack(a_kp)
    a_v_prompt = jnp.stack(a_vp)
    a_k_sample = jnp.stack(a_ks)
    a_v_sample = jnp.stack(a_vs)
    b_state_prompt = jnp.stack(b_sp)
    b_state_sample = jnp.stack(b_ss)
    c_k_prompt = jnp.stack(c_kp)
    c_v_prompt = jnp.stack(c_vp)
    c_k_sample = jnp.stack(c_ks)
    c_v_sample = jnp.stack(c_vs)
    return (y_prompt, y_sample, a_k_prompt, a_v_prompt, a_k_sample, a_v_sample,
            b_state_prompt, b_state_sample, c_k_prompt, c_v_prompt, c_k_sample, c_v_sample)
```

```python
import numpy as np
from contextlib import ExitStack
import concourse.bass as bass
import concourse.mybir as mybir
from concourse.bass_utils import run_bass_kernel_spmd

F32 = mybir.dt.float32
BF16 = mybir.dt.bfloat16
AF = mybir.ActivationFunctionType
ALU = mybir.AluOpType

T = 2048
TS = 16
TT = T + TS
KC = 8
TCH = [(0, 512), (512, 512), (1024, 512), (1536, 512), (2048, 16)]
TBLK = [(128 * j, 128) for j in range(16)] + [(2048, 16)]
EPS = 1e-6
DFF = 2816
NFF = DFF // 128
NW = 5
SB_FILL = 0


def chunk_of(lo):
    return min(lo // 512, 4)


class Buf:
    __slots__ = ("name", "w", "r", "excl")

    def __init__(self, name):
        self.name = name
        self.w = None
        self.r = []
        self.excl = False


class Eng:
    def __init__(self, fw, name, e, sem):
        self.fw = fw
        self.name = name
        self.e = e
        self.sem = sem
        self.count = 0
        self.seen = {}

    def wait(self, dep):
        if dep is None:
            return
        key, val, _ = dep
        if self.seen.get(key, 0) >= val:
            return
        self.e.wait_ge(self.fw.sems[key], val)
        self.seen[key] = val
        self.fw.nwaits += 1


class FW:
    def __init__(self, nc, stack, n_dma_sems=10):
        self.nc = nc
        self.sems = {}
        self.nwaits = 0
        self.engs = {}
        for name, e in (("pe", nc.tensor), ("act", nc.scalar), ("dve", nc.vector),
                        ("pool", nc.gpsimd), ("sp", nc.sync)):
            s = stack.enter_context(nc.semaphore("s_" + name))
            self.sems[name] = s
            self.engs[name] = Eng(self, name, e, s)
        self.dma_pool = {}
        for q in ("sp", "pool"):
            lst = []
            for i in range(n_dma_sems):
                key = "d_%s_%d" % (q, i)
                self.sems[key] = stack.enter_context(nc.semaphore(key))
                lst.append([key, 0])
            self.dma_pool[q] = [lst, 0]
        self.ninstr = 0
        self.dead = False

    def _deps(self, eng, reads, writes):
        for b in reads:
            if b.w is not None:
                eng.wait(b.w)
        strict = eng.name != "pe"
        for b in writes:
            if b.w is not None and (strict or b.w[2] != eng.name):
                eng.wait(b.w)
            for d in b.r:
                if strict or d[2] != eng.name:
                    eng.wait(d)

    def _record(self, dep, reads, writes):
        for b in reads:
            b.r.append(dep)
            if len(b.r) > 16:
                m = {}
                for d in b.r:
                    if d[0] not in m or m[d[0]][1] < d[1]:
                        m[d[0]] = d
                b.r = list(m.values())
        for b in writes:
            b.w = dep
            b.r = []

    def op(self, engname, fn, reads=(), writes=(), inc=True):
        if self.dead:
            return None
        eng = self.engs[engname]
        ex = [b for b in reads if b.excl]
        if ex:
            reads = [b for b in reads if not b.excl]
            writes = list(writes) + [b for b in ex if b not in writes]
        self._deps(eng, reads, writes)
        ins = fn(eng.e)
        self.ninstr += 1
        if inc:
            eng.count += 1
            ins.then_inc(eng.sem, 1)
            dep = (engname, eng.count, engname)
        else:
            dep = (engname, eng.count + 1, engname)
        self._record(dep, reads, writes)
        return ins

    def dma(self, q, out, in_, reads=(), writes=(), **kw):
        if self.dead:
            return None
        eng = self.engs[q]
        pool, idx = self.dma_pool[q]
        ent = pool[idx % len(pool)]
        self.dma_pool[q][1] = idx + 1
        key, val = ent
        if val > 0:
            eng.wait((key, val, "dma"))
        self._deps(eng, reads, writes)
        ins = eng.e.dma_start(out=out, in_=in_, **kw)
        ent[1] = val + 16
        ins.then_inc(self.sems[key], 16)
        self.ninstr += 1
        dep = (key, val + 16, "dma_" + key)
        self._record(dep, reads, writes)
        return dep

    def all_deps(self):
        deps = []
        for name, eng in self.engs.items():
            if eng.count > 0:
                deps.append((name, eng.count, name))
        for q, (pool, idx) in self.dma_pool.items():
            for key, val in pool:
                if val > 0:
                    deps.append((key, val, "dma"))
        return deps

    def barrier(self):
        if self.dead:
            return
        deps = self.all_deps()
        for name, eng in self.engs.items():
            for d in deps:
                if d[0] != name:
                    eng.wait(d)

    def finish_all(self):
        eng = self.engs["sp"]
        for d in self.all_deps():
            if d[0] != "sp":
                eng.wait(d)


class Tl:
    def __init__(self, t, name, nb=1):
        self.t = t
        self.bs = [Buf("%s%d" % (name, i)) for i in range(nb)]
        self.b = self.bs[0]


class Ring:
    def __init__(self, tiles):
        self.tiles = tiles
        self.i = 0

    def get(self):
        t = self.tiles[self.i % len(self.tiles)]
        self.i += 1
        return t


class _Stop(Exception):
    pass


def build_nc(NL=4, stop=None, debug=False):
    def ck(name):
        f_ = fwbox[0]
        build_nc.marks.append((name, {k: e.count for k, e in f_.engs.items()}))
        if stop == name:
            f_.dead = True
    fwbox = [None]
    build_nc.marks = []
    nc = bass.Bass("TRN2", target_bir_lowering=False)

    def din(name, shape):
        return nc.dram_tensor(name, list(shape), F32, kind="ExternalInput").ap()

    def dout(name, shape):
        return nc.dram_tensor(name, list(shape), F32, kind="ExternalOutput").ap()

    x_prompt = din("x_prompt", [T, 1024])
    x_sample = din("x_sample", [TS, 1024])
    cache_a_k = din("cache_a_k", [2, T, 512])
    cache_a_v = din("cache_a_v", [2, T, 512])
    state_b = din("state_b", [2, 4, 64, 128])
    cache_c_k = din("cache_c_k", [2, 512, 1024])
    cache_c_v = din("cache_c_v", [2, 512, 1024])
    norm_mix_g = din("norm_mix_g", [4, 1024])
    norm_ffn_g = din("norm_ffn_g", [4, 1024])
    w_in_ab = din("w_in_ab", [2, 1024, 3088])
    w_gate_b = din("w_gate_b", [2, 16, 256])
    b_gate_b = din("b_gate_b", [2, 256])
    norm_gla_g = din("norm_gla_g", [2, 512])
    w_out_ab = din("w_out_ab", [2, 1024, 1024])
    w_qkv_c = din("w_qkv_c", [2, 1024, 3072])
    rel_bias_c = din("rel_bias_c", [2, 16, 192])
    w_out_c = din("w_out_c", [2, 1024, 1024])
    w_ffn_gate = din("w_ffn_gate", [4, 1024, DFF])
    w_ffn_up = din("w_ffn_up", [4, 1024, DFF])
    w_ffn_down = din("w_ffn_down", [4, DFF, 1024])
    norm_final_g = din("norm_final_g", [1024])

    y_prompt = dout("y_prompt", [T, 1024])
    y_sample = dout("y_sample", [TS, 1024])
    a_k_prompt = dout("a_k_prompt", [2, T, 512])
    a_v_prompt = dout("a_v_prompt", [2, T, 512])
    a_k_sample = dout("a_k_sample", [2, TS, 512])
    a_v_sample = dout("a_v_sample", [2, TS, 512])
    b_state_prompt = dout("b_state_prompt", [2, 4, 64, 128])
    b_state_sample = dout("b_state_sample", [2, 4, 64, 128])
    c_k_prompt = dout("c_k_prompt", [2, 512, 1024])
    c_v_prompt = dout("c_v_prompt", [2, 512, 1024])
    c_k_sample = dout("c_k_sample", [2, TS, 1024])
    c_v_sample = dout("c_v_sample", [2, TS, 1024])
    if debug:
        dbg_x = dout("dbg_x", [128, KC, TT])
        dbg_h = dout("dbg_h", [128, KC, TT])
        dbg_c = dout("dbg_c", [128, KC, TT])
    scr = nc.dram_tensor("scr_rel", [2, 16, 512], F32, kind="Internal").ap()
    scr_b = Buf("scr")

    with ExitStack() as st:
        fw = FW(nc, st)
        fwbox[0] = fw

        uniq = [0]

        def sb(name, shape, dt, nb=1, stack=None):
            uniq[0] += 1
            name = "%s_u%d" % (name, uniq[0])
            t = (stack or st).enter_context(nc.sbuf_tensor(name, list(shape), dt))
            return Tl(t, name, nb)

        def ring(name, shape, dt, n, stack=None):
            return Ring([sb("%s_%d" % (name, i), shape, dt, stack=stack) for i in range(n)])

        xT = sb("xT", [128, KC, TT], F32, nb=5)
        hT = sb("hT", [128, KC, TT], BF16, nb=5)
        catT = sb("catT", [128, KC, TT], BF16, nb=5)
        wring = ring("wb", [128, 2048], BF16, NW)
        psr = Ring([Tl(st.enter_context(nc.psum_tensor("ps%d" % i, [128, 512], F32)), "ps%d" % i)
                    for i in range(8)])
        for _t in psr.tiles:
            _t.b.excl = True
        ident = sb("ident", [128, 128], F32)
        ones_bf = sb("ones_bf", [128, 128], BF16)
        triS = sb("triS", [128, 128], BF16)
        ident_bf = sb("ident_bf", [128, 128], BF16)
        mneg = sb("mneg", [128, 128], BF16)
        triIncN = sb("triIncN", [128, 128], F32)
        triRevN = sb("triRevN", [128, 128], F32)
        mInc = sb("mInc", [128, 128], BF16)
        mask0 = sb("mask0", [128, 128], F32)
        maskA = sb("maskA", [128, 128], F32)
        antiJ = sb("antiJ", [128, 128], F32)
        gains = sb("gains", [128, 80], F32)
        sqr = ring("sq", [128, 512], BF16, 2)
        f32r = ring("f32t", [128, 512], F32, 2)
        rstd_r = ring("rstd", [128, 512], F32, 1)
        stg = ring("stg", [128, 1024], F32, 1)
        kvst = ring("kvst", [128, 128], F32, 2)

        def P(hold=False):
            while True:
                t_ = psr.get()
                if not getattr(t_, "held", False):
                    break
            t_.held = hold
            return t_

        def mm(out, lhsT, rhs, start, stop, R, W, **kw):
            fw.op("pe", lambda e: e.matmul(out, lhsT, rhs, start=start, stop=stop, **kw),
                  reads=R, writes=W, inc=True)

        def tr(out, in_, idn, R, W):
            fw.op("pe", lambda e: e.matmul(out, in_, idn, start=True, stop=True, is_transpose=True), reads=R, writes=W)

        def act(out, in_, func, R, W, **kw):
            fw.op("act", lambda e: e.activation(out, in_, func, **kw), reads=R, writes=W)

        def tt(out, in0, in1, op, R, W, eng="dve"):
            fw.op(eng, lambda e: e.tensor_tensor(out, in0, in1, op), reads=R, writes=W)

        def stt(out, in0, scalar, in1, op0, op1, R, W):
            fw.op("dve", lambda e: e.scalar_tensor_tensor(out, in0, scalar, in1, op0, op1), reads=R, writes=W)

        def tsc(out, in0, s1, op0, R, W, s2=None, op1=None, eng="dve"):
            if op1 is None:
                fw.op(eng, lambda e: e.tensor_scalar(out, in0, s1, None, op0), reads=R, writes=W)
            else:
                fw.op(eng, lambda e: e.tensor_scalar(out, in0, s1, s2, op0, op1), reads=R, writes=W)

        def cp(out, in_, R, W, eng="dve"):
            if eng == "act":
                fw.op("act", lambda e: e.copy(out, in_), reads=R, writes=W)
            else:
                fw.op(eng, lambda e: e.tensor_copy(out, in_), reads=R, writes=W)

        def ms(ap, val, W, eng="dve"):
            fw.op(eng, lambda e: e.memset(ap, val), writes=W)

        def asel(tl, pattern, cmp, base, cm):
            fw.op("pool", lambda e: e.affine_select(out=tl.t[:], in_=tl.t[:], pattern=pattern, compare_op=cmp,
                                                    fill=0.0, base=base, channel_multiplier=cm),
                  reads=[tl.b], writes=[tl.b])

        ms(ident.t[:], 1.0, [ident.b], "pool")
        asel(ident, [[-1, 128]], ALU.is_equal, 0, 1)
        ms(ones_bf.t[:], 1.0, [ones_bf.b], "pool")
        ms(triS.t[:], 1.0, [triS.b], "pool")
        asel(triS, [[-1, 128]], ALU.is_gt, 0, 1)
        ms(ident_bf.t[:], 1.0, [ident_bf.b], "pool")
        asel(ident_bf, [[-1, 128]], ALU.is_equal, 0, 1)
        ms(mneg.t[:], -192.0, [mneg.b], "pool")
        asel(mneg, [[-1, 128]], ALU.is_ge, 0, 1)
        ms(triIncN.t[:], -1.0 / 16.0, [triIncN.b], "pool")
        asel(triIncN, [[1, 128]], ALU.is_ge, 0, -1)
        ms(triIncN.t[0:64, 64:128], 0.0, [triIncN.b], "pool")
        ms(triRevN.t[:], -1.0 / 16.0, [triRevN.b], "pool")
        asel(triRevN, [[-1, 128]], ALU.is_gt, 0, 1)
        ms(triRevN.t[64:128, 0:64], 0.0, [triRevN.b], "pool")
        ms(mInc.t[:], 1.0, [mInc.b], "pool")
        asel(mInc, [[1, 128]], ALU.is_ge, 0, -1)
        ms(mInc.t[0:64, 64:128], 0.0, [mInc.b], "pool")
        ms(mask0.t[:], 1.0, [mask0.b], "pool")
        ms(mask0.t[64:128, 0:64], 0.0, [mask0.b], "pool")
        ms(maskA.t[:], 1.0, [maskA.b], "pool")
        ms(maskA.t[0:64, 64:128], 0.0, [maskA.b], "pool")
        ms(antiJ.t[:], 1.0, [antiJ.b], "pool")
        asel(antiJ, [[1, 128]], ALU.is_equal, -127, 1)

        g_st = stg.get()
        fw.dma("sp", g_st.t[0:32, 0:128], norm_mix_g.rearrange("l (k p) -> (l k) p", p=128), writes=[g_st.b])
        fw.dma("sp", g_st.t[32:64, 0:128], norm_ffn_g.rearrange("l (k p) -> (l k) p", p=128), writes=[g_st.b])
        fw.dma("sp", g_st.t[64:72, 0:128], norm_final_g.rearrange("(k p) -> k p", p=128), writes=[g_st.b])
        fw.dma("sp", g_st.t[72:80, 0:128], norm_gla_g.rearrange("l (k p) -> (l k) p", p=128), writes=[g_st.b])
        ps = P()
        tr(ps.t[:, 0:80], g_st.t[0:80, 0:128], ident.t[0:80, 0:80], [g_st.b, ident.b], [ps.b])
        cp(gains.t[:, :], ps.t[:, 0:80], [ps.b], [gains.b])

        for (lo, m) in TBLK:
            xs = stg.get()
            src = x_prompt[lo:lo + m, :] if lo < T else x_sample[:, :]
            fw.dma("sp", xs.t[0:m, :], src, writes=[xs.b])
            cb = xT.bs[chunk_of(lo)]
            for half in range(2):
                ps = P()
                for q in range(4):
                    kc = half * 4 + q
                    tr(ps.t[:, q * 128:q * 128 + m], xs.t[0:m, kc * 128:(kc + 1) * 128], ident.t[0:m, 0:m],
                       [xs.b, ident.b], [ps.b])
                pv = ps.t[:, :].rearrange("p (q c) -> p q c", q=4)[:, :, 0:m]
                cp(xT.t[:, half * 4:half * 4 + 4, lo:lo + m], pv, [ps.b], [cb], eng=("act" if half else "dve"))

        wsched = []

        def colw(W2d, c0, ncol):
            wsched.append(("col", W2d, c0, ncol))

        def roww(W2d, r0, nk):
            wsched.append(("row", W2d, r0, nk))

        ffn_groups = [(0, 8), (8, 8), (16, 6)]
        for L in range(NL):
            i = L // 2
            if L % 2 == 0:
                W = w_in_ab[i]
                for hp in range(4):
                    colw(W, 128 * hp, 128)
                    colw(W, 512 + 128 * hp, 128)
                    colw(W, 1024 + 128 * hp, 128)
                colw(W, 2560, 16)
                for hf in range(2):
                    colw(W, 1536 + 128 * hf, 128)
                    colw(W, 1792 + 128 * hf, 128)
                    colw(W, 2048 + 256 * hf, 256)
                for h in range(4):
                    colw(W, 2576 + 128 * h, 128)
                for r in range(4):
                    roww(w_out_ab[i], 256 * r, 2)
            else:
                W = w_qkv_c[i]
                for hp in range(8):
                    colw(W, 128 * hp, 128)
                    colw(W, 1024 + 128 * hp, 128)
                    colw(W, 2048 + 128 * hp, 128)
                for r in range(4):
                    roww(w_out_c[i], 256 * r, 2)
            for (j0, nj) in ffn_groups:
                for j in range(j0, j0 + nj):
                    colw(w_ffn_gate[L], 128 * j, 128)
                    colw(w_ffn_up[L], 128 * j, 128)
                for r in range(nj // 2):
                    roww(w_ffn_down[L], 128 * j0 + 256 * r, 2)

        wstate = {"issued": 0, "used": 0, "tiles": {}, "live": []}

        def w_issue():
            k = wstate["issued"]
            kind, W2d, a, b = wsched[k]
            tl = wring.get()
            if kind == "col":
                dst = tl.t[:, 0:KC * b].rearrange("p (k c) -> p k c", k=KC)
                src = W2d.rearrange("(k p) c -> p k c", p=128)[:, :, a:a + b]
            else:
                dst = tl.t[:, 0:b * 1024].rearrange("p (k c) -> p k c", k=b)
                src = W2d[a:a + 128 * b, :].rearrange("(k p) c -> p k c", p=128)
            fw.dma("pool", dst, src, writes=[tl.b])
            wstate["tiles"][k] = (tl, dst)
            wstate["issued"] = k + 1

        def wget(kind, a, b):
            k = wstate["used"]
            assert wsched[k][0] == kind and wsched[k][2] == a and wsched[k][3] == b, (k, wsched[k], kind, a, b)
            oldest = wstate["live"][0] if wstate["live"] else k
            assert k < oldest + NW
            while wstate["issued"] < min(len(wsched), oldest + NW):
                w_issue()
            wstate["used"] = k + 1
            wstate["live"].append(k)
            tl, view = wstate["tiles"].pop(k)
            return tl, view

        def wfree():
            wstate["live"] = []

        def norm(gcol0, dst, dst_is_h=True):
            for ci, (lo, n) in enumerate(TCH):
                rs = norm_stats(ci)
                for kc in range(KC):
                    stt(dst.t[:, kc, lo:lo + n], xT.t[:, kc, lo:lo + n], gains.t[:, gcol0 + kc:gcol0 + kc + 1],
                        rs.t[:, 0:n], ALU.mult, ALU.mult, [xT.bs[ci], rs.b, gains.b], [dst.bs[ci]])

        def norm_stats(ci, sq_ring=None, rs_ring=None):
            sq_ring = sq_ring or sqr
            rs_ring = rs_ring or rstd_r
            lo, n = TCH[ci]
            ps = P()
            for kc in range(KC):
                sq = sq_ring.get()
                act(sq.t[:, 0:n], xT.t[:, kc, lo:lo + n], AF.Square, [xT.bs[ci]], [sq.b])
                mm(ps.t[:, 0:n], ones_bf.t[:, :], sq.t[:, 0:n], kc == 0, kc == KC - 1, [ones_bf.b, sq.b], [ps.b])
            rs = rs_ring.get()
            act(rs.t[:, 0:n], ps.t[:, 0:n], AF.Ln, [ps.b], [rs.b], scale=1.0 / 1024.0, bias=EPS)
            act(rs.t[:, 0:n], rs.t[:, 0:n], AF.Exp, [rs.b], [rs.b], scale=-0.5)
            return rs

        def proj_fm(wv, wb, ncol, evac):
            for ci, (lo, n) in enumerate(TCH):
                ps = P()
                for kc in range(KC):
                    mm(ps.t[0:ncol, 0:n], wv[:, kc, :], hT.t[:, kc, lo:lo + n], kc == 0, kc == KC - 1,
                       [wb, hT.bs[ci]], [ps.b])
                evac(ps, ci, lo, n)

        def proj_tm(wv, wb, ncol, blocks, evac):
            for bi in blocks:
                lo, m = TBLK[bi]
                ps = P()
                for kc in range(KC):
                    mm(ps.t[0:m, 0:ncol], hT.t[:, kc, lo:lo + m], wv[:, kc, :], kc == 0, kc == KC - 1,
                       [wb, hT.bs[chunk_of(lo)]], [ps.b])
                evac(ps, bi, lo, m)

        def out_proj(nrows_tiles, src):
            wts = [wget("row", 256 * r, 2) for r in range(4)]
            for f in range(KC):
                for ci, (lo, n) in enumerate(TCH):
                    ps = P()
                    for k in range(KC):
                        tl, wv = wts[k // 2]
                        mm(ps.t[:, 0:n], wv[:, k % 2, 128 * f:128 * f + 128], src.t[:, k, lo:lo + n],
                           k == 0, k == KC - 1, [tl.b, src.bs[ci]], [ps.b])
                    tt(xT.t[:, f, lo:lo + n], xT.t[:, f, lo:lo + n], ps.t[:, 0:n], ALU.add,
                       [xT.bs[ci], ps.b], [xT.bs[ci]])
            wfree()

        def ffn(L):
            norm(32 + 8 * L, hT)
            for (j0, nj) in ffn_groups:
                for jj in range(nj):
                    j = j0 + jj
                    gtl, gv = wget("col", 128 * j, 128)
                    utl, uv = wget("col", 128 * j, 128)
                    for ci, (lo, n) in enumerate(TCH):
                        pg = P()
                        pu = P()
                        for kc in range(KC):
                            mm(pg.t[:, 0:n], gv[:, kc, :], hT.t[:, kc, lo:lo + n], kc == 0, kc == KC - 1,
                               [gtl.b, hT.bs[ci]], [pg.b])
                        for kc in range(KC):
                            mm(pu.t[:, 0:n], uv[:, kc, :], hT.t[:, kc, lo:lo + n], kc == 0, kc == KC - 1,
                               [utl.b, hT.bs[ci]], [pu.b])
                        sg = f32r.get()
                        act(sg.t[:, 0:n], pg.t[:, 0:n], AF.Silu, [pg.b], [sg.b])
                        tt(catT.t[:, jj, lo:lo + n], sg.t[:, 0:n], pu.t[:, 0:n], ALU.mult, [sg.b, pu.b], [catT.bs[ci]])
                    wfree()
                wts = [wget("row", 128 * j0 + 256 * r, 2) for r in range(nj // 2)]
                for f in range(KC):
                    for ci, (lo, n) in enumerate(TCH):
                        ps = P()
                        for k in range(nj):
                            tl, wv = wts[k // 2]
                            mm(ps.t[:, 0:n], wv[:, k % 2, 128 * f:128 * f + 128], catT.t[:, k, lo:lo + n],
                               k == 0, k == nj - 1, [tl.b, catT.bs[ci]], [ps.b])
                        tt(xT.t[:, f, lo:lo + n], xT.t[:, f, lo:lo + n], ps.t[:, 0:n], ALU.add,
                           [xT.bs[ci], ps.b], [xT.bs[ci]])
                wfree()

        def sb_items(W, po, q_ap, qb, nq, blocks, out_ap, out_b):
            stc = {}
            nb = len(blocks)
            items = []
            for bi_, blk_ in enumerate(blocks):
                class It:
                    pass
                it = It()
                it.bi, it.blk = bi_, blk_
                it.next_qoff = blocks[bi_ + 1]["qoff"] if bi_ + 1 < nb else None

                def z(it=it):
                    blk, bi = it.blk, it.bi
                    if bi == 0:
                        stc["pv"] = P(hold=True)
                        stc["cacc"] = P(hold=True)
                        ms(stc["cacc"].t[0:1, 0:nq], 0.0, [stc["cacc"].b])
                    it.pv, it.cacc = stc["pv"], stc["cacc"]
                    nk, qoff, diag = blk["nk"], blk["qoff"], blk["diag"]
                    n = nq - qoff
                    it.zps = P()
                    mm(it.zps.t[0:nk, 0:n], blk["kT"], q_ap[:, qoff:qoff + n], True, True, blk["R"] + [qb], [it.zps.b])
                    if diag:
                        mm(it.zps.t[0:nk, 0:nk], ident_bf.t[0:nk, 0:nk], mneg.t[0:nk, 0:nk], False, True,
                           [ident_bf.b, mneg.b], [it.zps.b], skip_group_check=True)

                def front_act(it=it):
                    blk = it.blk
                    nk, qoff = blk["nk"], blk["qoff"]
                    n = nq - qoff
                    it.sp = W["sp"].get()
                    act(it.sp.t[0:nk, 0:n], it.zps.t[0:nk, 0:n], AF.Exp, [it.zps.b], [it.sp.b], scale=-0.125)
                    act(it.sp.t[0:nk, 0:n], it.sp.t[0:nk, 0:n], AF.Ln, [it.sp.b], [it.sp.b], bias=1.0)

                def front_dve(it=it):
                    blk, bi = it.blk, it.bi
                    nk, qoff = blk["nk"], blk["qoff"]
                    n = nq - qoff
                    it.Lt = W["L"].get()
                    stt(it.Lt.t[0:nk, 0:n], it.zps.t[0:nk, 0:n], -0.125, it.sp.t[0:nk, 0:n], ALU.mult, ALU.subtract,
                        [it.zps.b, it.sp.b], [it.Lt.b])

                def colsum(it=it):
                    blk, bi = it.blk, it.bi
                    nk, qoff = blk["nk"], blk["qoff"]
                    n = nq - qoff
                    if bi < nb - 1:
                        cacc = it.cacc
                        mm(cacc.t[0:1, qoff:qoff + n], ones_bf.t[0:nk, 0:1], it.Lt.t[0:nk, 0:n], False, True,
                           [ones_bf.b, it.Lt.b], [cacc.b], skip_group_check=True)

                def snap(it=it):
                    bi = it.bi
                    if bi < nb - 1:
                        cacc = it.cacc
                        q2 = it.next_qoff
                        cs = W["cset"].get()
                        it.cnext = cs
                        cp(cs.t[0:1, q2:nq], cacc.t[0:1, q2:nq], [cacc.b], [cs.b])

                def tri(it=it):
                    blk, bi = it.blk, it.bi
                    nk, qoff = blk["nk"], blk["qoff"]
                    n = nq - qoff
                    it.aps = P()
                    first = (bi == 0)
                    mm(it.aps.t[0:nk, 0:n], triS.t[0:nk, 0:nk], it.Lt.t[0:nk, 0:n], True, first, [triS.b, it.Lt.b], [it.aps.b])
                    if not first:
                        chi = items[bi - 1].cnext
                        mm(it.aps.t[0:nk, 0:n], ones_bf.t[0:1, 0:nk], chi.t[0:1, qoff:qoff + n], False, True,
                           [ones_bf.b, chi.b], [it.aps.b])

                def tmp(it=it):
                    blk = it.blk
                    nk, qoff = blk["nk"], blk["qoff"]
                    n = nq - qoff
                    tt(it.sp.t[0:nk, 0:n], it.aps.t[0:nk, 0:n], it.sp.t[0:nk, 0:n], ALU.subtract, [it.aps.b, it.sp.b], [it.sp.b])

                def expw(it=it):
                    blk = it.blk
                    nk, qoff = blk["nk"], blk["qoff"]
                    n = nq - qoff
                    it.wt = W["w"].get()
                    act(it.wt.t[0:nk, 0:n], it.sp.t[0:nk, 0:n], AF.Exp, [it.sp.b], [it.wt.b])

                def pvm(it=it):
                    blk, bi = it.blk, it.bi
                    nk, qoff = blk["nk"], blk["qoff"]
                    n = nq - qoff
                    pv = it.pv
                    mm(pv.t[:, qoff:qoff + n], blk["v"], it.wt.t[0:nk, 0:n], bi == 0, bi == nb - 1,
                       blk["VR"] + [it.wt.b], [pv.b], skip_group_check=True)
                    if bi == nb - 1:
                        cp(out_ap, pv.t[po:po + 64, 0:nq], [pv.b], [out_b], eng="act")
                        pv.held = False
                        it.cacc.held = False

                it.z, it.front_act, it.front_dve, it.tri, it.tmp, it.expw, it.pvm = z, front_act, front_dve, tri, tmp, expw, pvm
                it.colsum, it.snap = colsum, snap
                items.append(it)
            return items

        def sb_run(items, W):
            n = len(items)
            G = lambda j: items[j] if 0 <= j < n else None
            for i in range(-3, n + 1):
                a, b_, c, d = G(i + 3), G(i + 1), G(i), G(i - 1)
                if a is not None:
                    a.z()
                if b_ is not None:
                    b_.colsum()
                if c is not None:
                    c.tri()
                if d is not None:
                    d.pvm()
                for _f in range(SB_FILL):
                    fps = W["fill"]
                    fw.op("pe", lambda e: e.matmul(fps.t[:, 0:512], hT.t[:, 1, 0:128], hT.t[:, 0, 0:512], start=True, stop=True),
                          reads=[], writes=[], inc=False)
                if b_ is not None:
                    b_.snap()
                if c is not None:
                    c.tmp()
                if a is not None:
                    a.front_act()
                if c is not None:
                    c.expw()
                if a is not None:
                    a.front_dve()

        def layer_ab(i, L):
            norm(8 * L, hT)
            ck("norm")
            with ExitStack() as ph:
                qT2 = sb("qT2", [128, TT], BF16, stack=ph)
                kT2 = sb("kT2", [128, TT + T], BF16, stack=ph)
                v2 = sb("v2", [128, 33, 128], BF16, stack=ph)
                kst = ring("kst", [128, 2, 128], F32, 1, stack=ph)
                W = {"cset": ring("chi", [1, 512], BF16, 3, stack=ph),
                     "sp": ring("spr", [128, 512], F32, 4, stack=ph),
                     "L": ring("Lt", [128, 512], BF16, 4, stack=ph), "w": ring("wt", [128, 512], BF16, 3, stack=ph)}
                for hp in range(4):
                    qtl, qv = wget("col", 128 * hp, 128)
                    proj_fm(qv, qtl.b, 128, lambda ps, ci, lo, n: cp(qT2.t[:, lo:lo + n], ps.t[:, 0:n], [ps.b], [qT2.b], eng="act"))
                    ktl, kv = wget("col", 512 + 128 * hp, 128)
                    proj_fm(kv, ktl.b, 128, lambda ps, ci, lo, n: cp(kT2.t[:, lo:lo + n], ps.t[:, 0:n], [ps.b], [kT2.b], eng="act"))

                    def k_evac(ps, bi, lo, m):
                        s = kvst.get()
                        cp(s.t[0:m, 0:128], ps.t[0:m, 0:128], [ps.b], [s.b], eng="act")
                        dst = a_k_prompt[i, lo:lo + m, 128 * hp:128 * hp + 128] if lo < T else a_k_sample[i, :, 128 * hp:128 * hp + 128]
                        fw.dma("sp", dst, s.t[0:m, 0:128], reads=[s.b])
                    proj_tm(kv, ktl.b, 128, range(17), k_evac)
                    vtl, vv = wget("col", 1024 + 128 * hp, 128)

                    def v_evac(ps, bi, lo, m):
                        s = kvst.get()
                        cp(s.t[0:m, 0:128], ps.t[0:m, 0:128], [ps.b], [s.b], eng="act")
                        cp(v2.t[0:m, bi, :], ps.t[0:m, 0:128], [ps.b], [v2.b])
                        dst = a_v_prompt[i, lo:lo + m, 128 * hp:128 * hp + 128] if lo < T else a_v_sample[i, :, 128 * hp:128 * hp + 128]
                        fw.dma("sp", dst, s.t[0:m, 0:128], reads=[s.b])
                    proj_tm(vv, vtl.b, 128, range(17), v_evac)
                    wfree()
                    fw.dma("pool", v2.t[:, 17:33, :],
                           cache_a_v[i].rearrange("(b p) c -> p b c", p=128)[:, :, 128 * hp:128 * hp + 128],
                           writes=[v2.b])
                    for g in range(4):
                        ps = P()
                        for g2 in range(2):
                            ks = kst.get()
                            r0 = 512 * g + 256 * g2
                            fw.dma("sp", ks.t[:, :, :],
                                   cache_a_k[i, r0:r0 + 256, :].rearrange("(b p) c -> p b c", p=128)[:, :, 128 * hp:128 * hp + 128],
                                   writes=[ks.b])
                            for q in range(2):
                                qq = 2 * g2 + q
                                tr(ps.t[:, qq * 128:(qq + 1) * 128], ks.t[:, q, :], ident.t[:, :], [ks.b, ident.b], [ps.b])
                        c0 = TT + 512 * g
                        cp(kT2.t[:, c0:c0 + 512], ps.t[:, :], [ps.b], [kT2.b], eng="act")
                    ck("sbproj")
                    items = []
                    for hh in range(2):
                        po = 64 * hh
                        for c in range(4):
                            lo = 512 * c
                            blocks = []
                            for b in range(4 * c + 3, -1, -1):
                                blocks.append(dict(kT=kT2.t[po:po + 64, 128 * b:128 * b + 128], v=v2.t[:, b, :], nk=128,
                                                   qoff=max(0, 128 * b - lo), diag=(b >= 4 * c), R=[kT2.b], VR=[v2.b]))
                            items += sb_items(W, po, qT2.t[po:po + 64, lo:lo + 512], qT2.b, 512, blocks,
                                              catT.t[po:po + 64, hp, lo:lo + 512], catT.bs[c])
                        blocks = [dict(kT=kT2.t[po:po + 64, T:TT], v=v2.t[0:16, 16, :], nk=16, qoff=0, diag=True, R=[kT2.b], VR=[v2.b])]
                        for b in range(15, -1, -1):
                            blocks.append(dict(kT=kT2.t[po:po + 64, TT + 128 * b:TT + 128 * b + 128], v=v2.t[:, 17 + b, :], nk=128,
                                               qoff=0, diag=False, R=[kT2.b], VR=[v2.b]))
                        items += sb_items(W, po, qT2.t[po:po + 64, T:TT], qT2.b, 16, blocks,
                                          catT.t[po:po + 64, hp, T:TT], catT.bs[4])
                    W["fill"] = P(hold=True) if SB_FILL else None
                    sb_run(items, W)
                    if SB_FILL:
                        fw.op("pe", lambda e: e.matmul(W["fill"].t[:, 0:512], ones_bf.t[:, :], hT.t[:, 0, 0:512], start=True, stop=True),
                              reads=[], writes=[W["fill"].b])
                        W["fill"].held = False
                    ck("sbhp0")
                fw.barrier()
            ck("sb")
            gla(i, L)
            ck("gla")
            out_proj(4, catT)

        def gla(i, L):
            with ExitStack() as ph:
                glrT = sb("glrT", [16, TT], BF16, stack=ph)
                wg = sb("wg", [16, 256], BF16, stack=ph)
                bg = sb("bg", [1, 256], BF16, stack=ph)
                qg = sb("qg", [128, TT], BF16, stack=ph)
                kg = sb("kg", [128, TT], BF16, stack=ph)
                kd = sb("kd", [128, 17, 128], BF16, stack=ph)
                vt = sb("vt", [128, 17, 256], BF16, stack=ph)
                spg_r = ring("spg", [128, 128], F32, 2, stack=ph)
                ebl = sb("ebl", [128, 34], F32, stack=ph)
                Sf = sb("Sf", [128, 128], F32, stack=ph)
                Sb = sb("Sb", [128, 128], BF16, stack=ph)
                Sf2 = sb("Sf2", [128, 128], F32, stack=ph)
                Sb2 = sb("Sb2", [128, 128], BF16, stack=ph)
                att_r = ring("att", [128, 2, 128], BF16, 2, stack=ph)
                fw.dma("pool", wg.t[:, :], w_gate_b[i], writes=[wg.b])
                fw.dma("pool", bg.t[:, :], b_gate_b[i:i + 1, :], writes=[bg.b])
                gtl, gv = wget("col", 2560, 16)
                proj_fm(gv, gtl.b, 16, lambda ps, ci, lo, n: cp(glrT.t[0:16, lo:lo + n], ps.t[0:16, 0:n], [ps.b], [glrT.b], eng="act"))
                wfree()
                for hf in range(2):
                    qtl, qv = wget("col", 1536 + 128 * hf, 128)
                    ktl, kv = wget("col", 1792 + 128 * hf, 128)
                    vtl, vv = wget("col", 2048 + 256 * hf, 256)
                    for bi, (lo, m) in enumerate(TBLK):
                        hb = hT.bs[chunk_of(lo)]
                        ps = P()
                        mm(ps.t[0:m, 0:128], glrT.t[0:16, lo:lo + m], wg.t[0:16, 128 * hf:128 * hf + 128], True, False, [glrT.b, wg.b], [ps.b])
                        mm(ps.t[0:m, 0:128], ones_bf.t[0:1, 0:m], bg.t[0:1, 128 * hf:128 * hf + 128], False, True, [ones_bf.b, bg.b], [ps.b])
                        spg = spg_r.get()
                        act(spg.t[0:m, :], ps.t[0:m, 0:128], AF.Exp, [ps.b], [spg.b], scale=-1.0)
                        act(spg.t[0:m, :], spg.t[0:m, :], AF.Ln, [spg.b], [spg.b], bias=1.0)
                        pb = P()
                        mm(pb.t[:, 0:m], spg.t[0:m, :], triIncN.t[0:m, 0:m], True, True, [spg.b, triIncN.b], [pb.b])
                        eb = f32r.get()
                        act(eb.t[:, 0:m], pb.t[:, 0:m], AF.Exp, [pb.b], [eb.b])
                        act(eb.t[:, 128:128 + m], pb.t[:, 0:m], AF.Exp, [pb.b, eb.b], [eb.b], scale=-1.0)
                        if m == 128:
                            act(ebl.t[:, 2 * bi:2 * bi + 1], pb.t[:, 63:64], AF.Exp, [pb.b], [ebl.b])
                            act(ebl.t[:, 2 * bi + 1:2 * bi + 2], pb.t[:, 127:128], AF.Exp, [pb.b], [ebl.b])
                        else:
                            act(ebl.t[:, 32:33], pb.t[:, 15:16], AF.Exp, [pb.b], [ebl.b])
                        pq = P()
                        for kc in range(KC):
                            mm(pq.t[:, 0:m], qv[:, kc, :], hT.t[:, kc, lo:lo + m], kc == 0, kc == KC - 1, [qtl.b, hb], [pq.b])
                        stt(qg.t[:, lo:lo + m], pq.t[:, 0:m], 0.125, eb.t[:, 0:m], ALU.mult, ALU.mult, [pq.b, eb.b], [qg.b])
                        pk = P()
                        for kc in range(KC):
                            mm(pk.t[:, 0:m], kv[:, kc, :], hT.t[:, kc, lo:lo + m], kc == 0, kc == KC - 1, [ktl.b, hb], [pk.b])
                        tt(kg.t[:, lo:lo + m], pk.t[:, 0:m], eb.t[:, 128:128 + m], ALU.mult, [pk.b, eb.b], [kg.b])
                        pr = P()
                        mm(pr.t[0:m, 0:128], triRevN.t[0:m, 0:m], spg.t[0:m, :], True, True, [spg.b, triRevN.b], [pr.b])
                        er = f32r.get()
                        act(er.t[0:m, 0:128], pr.t[0:m, 0:128], AF.Exp, [pr.b], [er.b])
                        pkt = P()
                        for kc in range(KC):
                            mm(pkt.t[0:m, 0:128], hT.t[:, kc, lo:lo + m], kv[:, kc, :], kc == 0, kc == KC - 1, [ktl.b, hb], [pkt.b])
                        tt(kd.t[0:m, bi, :], pkt.t[0:m, 0:128], er.t[0:m, 0:128], ALU.mult, [pkt.b, er.b], [kd.b])
                        pvt = P()
                        for kc in range(KC):
                            mm(pvt.t[0:m, 0:256], hT.t[:, kc, lo:lo + m], vv[:, kc, :], kc == 0, kc == KC - 1, [vtl.b, hb], [pvt.b])
                        cp(vt.t[0:m, bi, :], pvt.t[0:m, 0:256], [pvt.b], [vt.b], eng="act")
                    wfree()
                    ck("glaproj")
                    ms(Sf.t[:, :], 0.0, [Sf.b])
                    ms(Sb.t[:, :], 0.0, [Sb.b])
                    for hh in range(2):
                        fw.dma("sp", Sf2.t[64 * hh:64 * hh + 64, :], state_b[i, 2 * hf + hh], writes=[Sf2.b])
                    cp(Sb2.t[:, :], Sf2.t[:, :], [Sf2.b], [Sb2.b])
                    ck("recA")
                    for bi, (lo, m) in enumerate(TBLK):
                        S_f, S_b = (Sf, Sb) if m == 128 else (Sf2, Sb2)
                        cb = catT.bs[chunk_of(lo)]
                        pas = [P(), P()]
                        for hh in range(2):
                            po = 64 * hh
                            mm(pas[hh].t[0:m, 0:m], kg.t[po:po + 64, lo:lo + m], qg.t[po:po + 64, lo:lo + m],
                               True, True, [kg.b, qg.b], [pas[hh].b])
                        at = att_r.get()
                        for hh in range(2):
                            tt(at.t[0:m, hh, 0:m], pas[hh].t[0:m, 0:m], mInc.t[0:m, 0:m], ALU.mult,
                               [pas[hh].b, mInc.b], [at.b])
                        ck("recB")
                        pos = [P(), P()]
                        nsub = 2 if m == 128 else 1
                        L_ = 64 if m == 128 else 16
                        for sub in range(nsub):
                            c0 = sub * L_
                            for hh in range(2):
                                po = 64 * hh
                                po_ps = pos[hh]
                                if sub == 0:
                                    mm(po_ps.t[:, 0:m], vt.t[0:m, bi, 128 * hh:128 * hh + 128], at.t[0:m, hh, 0:m],
                                       True, False, [vt.b, at.b], [po_ps.b], skip_group_check=True)
                                mm(po_ps.t[:, c0:c0 + L_], S_b.t[po:po + 64, :], qg.t[po:po + 64, lo + c0:lo + c0 + L_],
                                   False, (sub == nsub - 1), [S_b.b, qg.b], [po_ps.b], skip_group_check=True)
                            ck("recC")
                            pss = P()
                            for hh in range(2):
                                mm(pss.t[:, 128 * hh:128 * hh + 128], kd.t[c0:c0 + L_, bi, :], vt.t[c0:c0 + L_, bi, 128 * hh:128 * hh + 128],
                                   True, True, [kd.b, vt.b], [pss.b])
                            cidx = (2 * bi + sub) if m == 128 else 32
                            for hh in range(2):
                                po = 64 * hh
                                stt(S_f.t[po:po + 64, :], S_f.t[po:po + 64, :], ebl.t[po:po + 64, cidx:cidx + 1],
                                    pss.t[po:po + 64, 128 * hh:128 * hh + 128], ALU.mult, ALU.add, [S_f.b, ebl.b, pss.b], [S_f.b])
                            cp(S_b.t[:, :], S_f.t[:, :], [S_f.b], [S_b.b], eng="act")
                            ck("recD")
                        for hh in range(2):
                            cp(catT.t[:, 4 + 2 * hf + hh, lo:lo + m], pos[hh].t[:, 0:m], [pos[hh].b], [cb], eng="act")
                        ck("rec1")
                    for hh in range(2):
                        fw.dma("sp", b_state_prompt[i, 2 * hf + hh], Sf.t[64 * hh:64 * hh + 64, :], reads=[Sf.b])
                        fw.dma("sp", b_state_sample[i, 2 * hf + hh], Sf2.t[64 * hh:64 * hh + 64, :], reads=[Sf2.b])
                ck("rec")
                for h in range(4):
                    rtl, rv = wget("col", 2576 + 128 * h, 128)
                    for ci, (lo, n) in enumerate(TCH):
                        cb = catT.bs[ci]
                        sq = sqr.get()
                        act(sq.t[:, 0:n], catT.t[:, 4 + h, lo:lo + n], AF.Square, [cb], [sq.b])
                        pss = P()
                        mm(pss.t[:, 0:n], ones_bf.t[:, :], sq.t[:, 0:n], True, True, [ones_bf.b, sq.b], [pss.b])
                        rs = rstd_r.get()
                        act(rs.t[:, 0:n], pss.t[:, 0:n], AF.Ln, [pss.b], [rs.b], scale=1.0 / 128.0, bias=EPS)
                        act(rs.t[:, 0:n], rs.t[:, 0:n], AF.Exp, [rs.b], [rs.b], scale=-0.5)
                        pr = P()
                        for kc in range(KC):
                            mm(pr.t[:, 0:n], rv[:, kc, :], hT.t[:, kc, lo:lo + n], kc == 0, kc == KC - 1, [rtl.b, hT.bs[ci]], [pr.b])
                        er = f32r.get()
                        act(er.t[:, 0:n], pr.t[:, 0:n], AF.Exp, [pr.b], [er.b], scale=-1.0)
                        act(er.t[:, 0:n], er.t[:, 0:n], AF.Ln, [er.b], [er.b], bias=1.0)
                        act(er.t[:, 0:n], er.t[:, 0:n], AF.Exp, [er.b], [er.b], scale=-1.0)
                        tt(er.t[:, 0:n], er.t[:, 0:n], pr.t[:, 0:n], ALU.mult, [er.b, pr.b], [er.b])
                        stt(rs.t[:, 0:n], rs.t[:, 0:n], gains.t[:, 72 + 4 * i + h:73 + 4 * i + h], er.t[:, 0:n],
                            ALU.mult, ALU.mult, [rs.b, gains.b, er.b], [rs.b])
                        tt(catT.t[:, 4 + h, lo:lo + n], catT.t[:, 4 + h, lo:lo + n], rs.t[:, 0:n], ALU.mult, [cb, rs.b], [cb])
                    wfree()
                fw.barrier()

        def layer_c(i, L):
            norm(8 * L, hT)
            with ExitStack() as ph:
                qT2 = sb("cqT2", [128, TT], BF16, stack=ph)
                kT2 = sb("ckT2", [128, TT + 512], BF16, stack=ph)
                v2 = sb("cv2", [128, 21, 128], BF16, stack=ph)
                kst = sb("ckst", [128, 4, 128], F32, stack=ph)
                ext = sb("ext", [16, 512], F32, stack=ph)
                Gst = ring("Gst", [128, 3, 128], F32, 2, stack=ph)
                Mh = sb("Mh", [128, 2, 5, 128], BF16, stack=ph)
                Pr = ring("Pr", [128, 5, 128], BF16, 3, stack=ph)
                rden_r = ring("rden", [128, 128], F32, 2, stack=ph)
                fw.dma("sp", ext.t[:, 64:256], rel_bias_c[i], writes=[ext.b])
                cp(ext.t[:, 0:64], ext.t[:, 64:65].to_broadcast([16, 64]), [ext.b], [ext.b])
                cp(ext.t[:, 256:512], ext.t[:, 255:256].to_broadcast([16, 256]), [ext.b], [ext.b])
                act(ext.t[:, :], ext.t[:, :], AF.Exp, [ext.b], [ext.b])
                fw.dma("sp", scr[i], ext.t[:, :], reads=[ext.b], writes=[scr_b])
                ck("cext")
                for hp in range(8):
                    qtl, qv = wget("col", 128 * hp, 128)
                    proj_fm(qv, qtl.b, 128, lambda ps, ci, lo, n: cp(qT2.t[:, lo:lo + n], ps.t[:, 0:n], [ps.b], [qT2.b], eng="act"))
                    ktl, kv = wget("col", 1024 + 128 * hp, 128)
                    proj_fm(kv, ktl.b, 128, lambda ps, ci, lo, n: cp(kT2.t[:, lo:lo + n], ps.t[:, 0:n], [ps.b], [kT2.b], eng="act"))

                    def k_evac(ps, bi, lo, m):
                        s = kvst.get()
                        cp(s.t[0:m, 0:128], ps.t[0:m, 0:128], [ps.b], [s.b], eng="act")
                        dst = c_k_prompt[i, lo - 1536:lo - 1536 + m, 128 * hp:128 * hp + 128] if lo < T else c_k_sample[i, :, 128 * hp:128 * hp + 128]
                        fw.dma("sp", dst, s.t[0:m, 0:128], reads=[s.b])
                    ck("cqk")
                    proj_tm(kv, ktl.b, 128, range(12, 17), k_evac)
                    ck("ckev")
                    vtl, vv = wget("col", 2048 + 128 * hp, 128)

                    def v_evac(ps, bi, lo, m):
                        cp(v2.t[0:m, bi, :], ps.t[0:m, 0:128], [ps.b], [v2.b])
                        if bi >= 12:
                            s = kvst.get()
                            cp(s.t[0:m, 0:128], ps.t[0:m, 0:128], [ps.b], [s.b], eng="act")
                            dst = c_v_prompt[i, lo - 1536:lo - 1536 + m, 128 * hp:128 * hp + 128] if lo < T else c_v_sample[i, :, 128 * hp:128 * hp + 128]
                            fw.dma("sp", dst, s.t[0:m, 0:128], reads=[s.b])
                    proj_tm(vv, vtl.b, 128, range(17), v_evac)
                    ck("cv")
                    wfree()
                    fw.dma("pool", v2.t[:, 17:21, :],
                           cache_c_v[i].rearrange("(b p) c -> p b c", p=128)[:, :, 128 * hp:128 * hp + 128], writes=[v2.b])
                    ck("cvc")
                    fw.dma("sp", kst.t[:, :, :],
                           cache_c_k[i].rearrange("(b p) c -> p b c", p=128)[:, :, 128 * hp:128 * hp + 128], writes=[kst.b])
                    ps = P()
                    for q in range(4):
                        tr(ps.t[:, q * 128:(q + 1) * 128], kst.t[:, q, :], ident.t[:, :], [kst.b, ident.b], [ps.b])
                    cp(kT2.t[:, TT:TT + 512], ps.t[:, :], [ps.b], [kT2.b], eng="act")
                    ck("ccache")
                    for hh in range(2):
                        h = 2 * hp + hh
                        G = Gst.get()
                        base = (i * 16 + h) * 512
                        fw.dma("sp", G.t[:, 0, :], bass.AP(tensor=scr.tensor, offset=base + 0, ap=[[1, 128], [1, 128]]), reads=[scr_b], writes=[G.b])
                        fw.dma("sp", G.t[:, 1, :], bass.AP(tensor=scr.tensor, offset=base + 128, ap=[[1, 128], [1, 128]]), reads=[scr_b], writes=[G.b])
                        fw.dma("sp", G.t[:, 2, :], bass.AP(tensor=scr.tensor, offset=base + 300, ap=[[0, 128], [1, 128]]), reads=[scr_b], writes=[G.b])
                        pe0 = P()
                        mm(pe0.t[:, 0:128], antiJ.t[:, :], G.t[:, 0, :], True, True, [antiJ.b, G.b], [pe0.b])
                        mm(pe0.t[:, 128:256], antiJ.t[:, :], G.t[:, 1, :], True, True, [antiJ.b, G.b], [pe0.b])
                        tt(Mh.t[:, hh, 4, :], pe0.t[:, 0:128], mask0.t[:, :], ALU.mult, [pe0.b, mask0.b], [Mh.b])
                        cp(Mh.t[:, hh, 3, :], pe0.t[:, 128:256], [pe0.b], [Mh.b])
                        cp(Mh.t[:, hh, 2, :], G.t[:, 2, :], [G.b], [Mh.b])
                        cp(Mh.t[:, hh, 1, :], G.t[:, 2, :], [G.b], [Mh.b])
                        tt(Mh.t[:, hh, 0, :], G.t[:, 2, :], maskA.t[:, :], ALU.mult, [G.b, maskA.b], [Mh.b])
                    ck("cproj")
                    citems = []
                    for hh in range(2):
                        po = 64 * hh
                        for bi, (lo, m) in enumerate(TBLK):
                            if m == 128:
                                srcs = []
                                for s_ in range(5):
                                    kb = bi - 4 + s_
                                    if kb >= 0:
                                        srcs.append((s_, kT2.t[po:po + 64, 128 * kb:128 * kb + 128], v2.t[:, kb, :], 128))
                            else:
                                srcs = [(s_, kT2.t[po:po + 64, TT + 128 * s_:TT + 128 * s_ + 128], v2.t[:, 17 + s_, :], 128) for s_ in range(4)]
                                srcs.append((4, kT2.t[po:po + 64, T:TT], v2.t[0:16, 16, :], 16))

                            class CIt:
                                pass
                            it = CIt()

                            def cz(it=it, srcs=srcs, po=po, lo=lo, m=m):
                                it.pz = P()
                                it.pz2 = P()
                                for (s_, kap, vap, nk) in srcs:
                                    tgt = it.pz.t[0:nk, 128 * s_:128 * s_ + m] if s_ < 4 else it.pz2.t[0:nk, 0:m]
                                    mm(tgt, kap, qT2.t[po:po + 64, lo:lo + m], True, True, [kT2.b, qT2.b], [it.pz.b if s_ < 4 else it.pz2.b])

                            def cexp(it=it, srcs=srcs, m=m):
                                pz, pz2 = it.pz, it.pz2
                                s0 = srcs[0][0]
                                it.Pt = Pt = Pr.get()
                                if s0 < 4:
                                    if m == 128:
                                        act(Pt.t[:, s0:4, :], pz.t[:, 128 * s0:512].rearrange("p (s c) -> p s c", c=128), AF.Exp,
                                            [pz.b], [Pt.b], scale=0.125)
                                    else:
                                        act(Pt.t[:, 0:4, 0:m], pz.t[:, :].rearrange("p (s c) -> p s c", c=128)[:, :, 0:m], AF.Exp,
                                            [pz.b], [Pt.b], scale=0.125)
                                nk4 = srcs[-1][3]
                                act(Pt.t[0:nk4, 4, 0:m], pz2.t[0:nk4, 0:m], AF.Exp, [pz2.b], [Pt.b], scale=0.125)

                            def cmult(it=it, srcs=srcs, m=m, hh=hh):
                                Pt = it.Pt
                                s0 = srcs[0][0]
                                if m == 128:
                                    tt(Pt.t[:, s0:5, :], Pt.t[:, s0:5, :], Mh.t[:, hh, s0:5, :], ALU.mult, [Pt.b, Mh.b], [Pt.b])
                                else:
                                    tt(Pt.t[:, 0:4, 0:m], Pt.t[:, 0:4, 0:m], Mh.t[:, hh, 0:4, 0:m], ALU.mult, [Pt.b, Mh.b], [Pt.b])
                                    tt(Pt.t[0:16, 4, 0:m], Pt.t[0:16, 4, 0:m], Mh.t[0:16, hh, 4, 0:m], ALU.mult, [Pt.b, Mh.b], [Pt.b])

                            def cpv(it=it, srcs=srcs, m=m):
                                Pt = it.Pt
                                it.pvv = pvv = P()
                                it.pdn = pdn = P()
                                for k_, (s_, kap, vap, nk) in enumerate(srcs):
                                    mm(pvv.t[:, 0:m], vap, Pt.t[0:nk, s_, 0:m], k_ == 0, k_ == len(srcs) - 1, [v2.b, Pt.b], [pvv.b])
                                for k_, (s_, kap, vap, nk) in enumerate(srcs):
                                    mm(pdn.t[:, 0:m], ones_bf.t[0:nk, :], Pt.t[0:nk, s_, 0:m], k_ == 0, k_ == len(srcs) - 1, [ones_bf.b, Pt.b], [pdn.b])

                            def cnorm_a(it=it, po=po, lo=lo, m=m, hp=hp):
                                pdn = it.pdn
                                it.rd = rd = rden_r.get()
                                act(rd.t[po:po + 64, 0:m], pdn.t[po:po + 64, 0:m], AF.Ln, [pdn.b], [rd.b])
                                act(rd.t[po:po + 64, 0:m], rd.t[po:po + 64, 0:m], AF.Exp, [rd.b], [rd.b], scale=-1.0)

                            def cnorm(it=it, po=po, lo=lo, m=m, hp=hp):
                                pvv, rd = it.pvv, it.rd
                                tt(catT.t[po:po + 64, hp, lo:lo + m], pvv.t[po:po + 64, 0:m], rd.t[po:po + 64, 0:m], ALU.mult,
                                   [pvv.b, rd.b], [catT.bs[chunk_of(lo)]])

                            it.cnorm_a = cnorm_a
                            it.cz, it.cexp, it.cmult, it.cpv, it.cnorm = cz, cexp, cmult, cpv, cnorm
                            citems.append(it)
                    nci = len(citems)
                    citems[0].cz()
                    citems[0].cexp()
                    for ii in range(nci):
                        if ii + 1 < nci:
                            citems[ii + 1].cz()
                        citems[ii].cmult()
                        if ii >= 1:
                            citems[ii - 1].cnorm_a()
                        if ii + 1 < nci:
                            citems[ii + 1].cexp()
                        citems[ii].cpv()
                        if ii >= 1:
                            citems[ii - 1].cnorm()
                    citems[nci - 1].cnorm_a()
                    citems[nci - 1].cnorm()
                    ck("chp0")
                fw.barrier()
            out_proj(4, catT)

        ck("xload")
        for L in range(NL):
            if L % 2 == 0:
                layer_ab(L // 2, L)
            else:
                layer_c(L // 2, L)
            ck("mix%d" % L)
            ffn(L)
            ck("ffn%d" % L)

        if debug:
            fw.dead = False
            fw.barrier()
            fw.dma("sp", dbg_x, xT.t[:, :, :], reads=xT.bs)
            fw.dma("pool", dbg_h, hT.t[:, :, :], reads=hT.bs)
            fw.dma("pool", dbg_c, catT.t[:, :, :], reads=catT.bs)
            if stop is not None:
                fw.dead = True
        yt_r = ring("ytmp", [128, 8, 128], F32, 2)
        for ci, (lo, n) in enumerate(TCH):
            rs = norm_stats(ci)
            for b0 in range(0, n, 128):
                m = min(128, n - b0)
                yt = yt_r.get()
                for kc in range(KC):
                    stt(yt.t[:, kc, 0:m], xT.t[:, kc, lo + b0:lo + b0 + m], gains.t[:, 64 + kc:65 + kc], rs.t[:, b0:b0 + m],
                        ALU.mult, ALU.mult, [xT.bs[ci], rs.b, gains.b], [yt.b])
                ys = stg.get()
                for half in range(2):
                    ps = P()
                    for q in range(4):
                        tr(ps.t[0:m, q * 128:(q + 1) * 128], yt.t[:, half * 4 + q, 0:m], ident.t[:, :], [yt.b, ident.b], [ps.b])
                    cp(ys.t[0:m, half * 512:half * 512 + 512], ps.t[0:m, :], [ps.b], [ys.b], eng=("act" if half else "dve"))
                dst = y_prompt[lo + b0:lo + b0 + m, :] if lo < T else y_sample[:, :]
                fw.dma("sp", dst, ys.t[0:m, :], reads=[ys.b])
        fw.dead = False
        fw.finish_all()
        build_nc.stats = (fw.ninstr, fw.nwaits)
    return nc


IN_NAMES = ["x_prompt", "x_sample", "cache_a_k", "cache_a_v", "state_b", "cache_c_k", "cache_c_v",
            "norm_mix_g", "norm_ffn_g", "w_in_ab", "w_gate_b", "b_gate_b", "norm_gla_g", "w_out_ab",
            "w_qkv_c", "rel_bias_c", "w_out_c", "w_ffn_gate", "w_ffn_up", "w_ffn_down", "norm_final_g"]


def make_in_maps(inputs, n=8):
    f = lambda a: np.ascontiguousarray(np.asarray(a, dtype=np.float32))
    maps = []
    shared = {k: f(inputs[k]) for k in IN_NAMES[7:]}
    for b in range(n):
        m = dict(shared)
        m["x_prompt"] = f(inputs["x_prompt"][b])
        m["x_sample"] = f(inputs["x_sample"][b])
        m["cache_a_k"] = f(np.asarray(inputs["cache_a_k"])[:, b].reshape(2, T, 512))
        m["cache_a_v"] = f(np.asarray(inputs["cache_a_v"])[:, b].reshape(2, T, 512))
        m["state_b"] = f(np.asarray(inputs["state_b"])[:, b])
        m["cache_c_k"] = f(np.asarray(inputs["cache_c_k"])[:, b].reshape(2, 512, 1024))
        m["cache_c_v"] = f(np.asarray(inputs["cache_c_v"])[:, b].reshape(2, 512, 1024))
        maps.append(m)
    return maps


def gather(results):
    n = len(results)
    st = lambda k: np.stack([np.asarray(r[k], dtype=np.float32) for r in results], axis=0)
    y_prompt = st("y_prompt")
    y_sample = st("y_sample")
    ab = lambda k, tt_: np.transpose(st(k), (1, 0, 2, 3)).reshape(2, n, tt_, 8, 64)
    cc = lambda k, tt_: np.transpose(st(k), (1, 0, 2, 3)).reshape(2, n, tt_, 16, 64)
    bs = lambda k: np.transpose(st(k), (1, 0, 2, 3, 4))
    return (y_prompt, y_sample, ab("a_k_prompt", T), ab("a_v_prompt", T), ab("a_k_sample", TS), ab("a_v_sample", TS),
            bs("b_state_prompt"), bs("b_state_sample"), cc("c_k_prompt", 512), cc("c_v_prompt", 512),
            cc("c_k_sample", TS), cc("c_v_sample", TS))


def kernel(**inputs):
    nc = build_nc(4)
    in_maps = make_in_maps(inputs, 8)
    res = run_bass_kernel_spmd(nc, in_maps, core_ids=list(range(8)))
    return gather(res.results)
```

```python
import numpy as np
from contextlib import ExitStack
import concourse.bass as bass
import concourse.mybir as mybir
from concourse.bass_utils import run_bass_kernel_spmd

F32 = mybir.dt.float32
BF16 = mybir.dt.bfloat16
AF = mybir.ActivationFunctionType
ALU = mybir.AluOpType

T = 2048
TS = 16
TT = T + TS
KC = 8
TCH = [(0, 512), (512, 512), (1024, 512), (1536, 512), (2048, 16)]
TBLK = [(128 * j, 128) for j in range(16)] + [(2048, 16)]
EPS = 1e-6
DFF = 2816
NFF = DFF // 128
NW = 5
SB_FILL = 0


def chunk_of(lo):
    return min(lo // 512, 4)


class Buf:
    __slots__ = ("name", "w", "r", "excl")

    def __init__(self, name):
        self.name = name
        self.w = None
        self.r = []
        self.excl = False


class Eng:
    def __init__(self, fw, name, e, sem):
        self.fw = fw
        self.name = name
        self.e = e
        self.sem = sem
        self.count = 0
        self.seen = {}

    def wait(self, dep):
        if dep is None:
            return
        key, val, _ = dep
        if self.seen.get(key, 0) >= val:
            return
        self.e.wait_ge(self.fw.sems[key], val)
        self.seen[key] = val
        self.fw.nwaits += 1


class FW:
    def __init__(self, nc, stack, n_dma_sems=10):
        self.nc = nc
        self.sems = {}
        self.nwaits = 0
        self.engs = {}
        for name, e in (("pe", nc.tensor), ("act", nc.scalar), ("dve", nc.vector),
                        ("pool", nc.gpsimd), ("sp", nc.sync)):
            s = stack.enter_context(nc.semaphore("s_" + name))
            self.sems[name] = s
            self.engs[name] = Eng(self, name, e, s)
        self.dma_pool = {}
        for q in ("sp", "pool"):
            lst = []
            for i in range(n_dma_sems):
                key = "d_%s_%d" % (q, i)
                self.sems[key] = stack.enter_context(nc.semaphore(key))
                lst.append([key, 0])
            self.dma_pool[q] = [lst, 0]
        self.ninstr = 0
        self.dead = False

    def _deps(self, eng, reads, writes):
        for b in reads:
            if b.w is not None:
                eng.wait(b.w)
        strict = eng.name == "pool"
        for b in writes:
            if b.w is not None and (strict or b.w[2] != eng.name):
                eng.wait(b.w)
            for d in b.r:
                if strict or d[2] != eng.name:
                    eng.wait(d)

    def _record(self, dep, reads, writes):
        for b in reads:
            b.r.append(dep)
            if len(b.r) > 16:
                m = {}
                for d in b.r:
                    if d[0] not in m or m[d[0]][1] < d[1]:
                        m[d[0]] = d
                b.r = list(m.values())
        for b in writes:
            b.w = dep
            b.r = []

    def op(self, engname, fn, reads=(), writes=(), inc=True):
        if self.dead:
            return None
        eng = self.engs[engname]
        ex = [b for b in reads if b.excl]
        if ex:
            reads = [b for b in reads if not b.excl]
            writes = list(writes) + [b for b in ex if b not in writes]
        self._deps(eng, reads, writes)
        ins = fn(eng.e)
        self.ninstr += 1
        if inc:
            eng.count += 1
            ins.then_inc(eng.sem, 1)
            dep = (engname, eng.count, engname)
        else:
            dep = (engname, eng.count + 1, engname)
        self._record(dep, reads, writes)
        return ins

    def dma(self, q, out, in_, reads=(), writes=(), **kw):
        if self.dead:
            return None
        eng = self.engs[q]
        pool, idx = self.dma_pool[q]
        ent = pool[idx % len(pool)]
        self.dma_pool[q][1] = idx + 1
        key, val = ent
        if val > 0:
            eng.wait((key, val, "dma"))
        self._deps(eng, reads, writes)
        ins = eng.e.dma_start(out=out, in_=in_, **kw)
        ent[1] = val + 16
        ins.then_inc(self.sems[key], 16)
        self.ninstr += 1
        dep = (key, val + 16, "dma_" + key)
        self._record(dep, reads, writes)
        return dep

    def all_deps(self):
        deps = []
        for name, eng in self.engs.items():
            if eng.count > 0:
                deps.append((name, eng.count, name))
        for q, (pool, idx) in self.dma_pool.items():
            for key, val in pool:
                if val > 0:
                    deps.append((key, val, "dma"))
        return deps

    def barrier(self):
        if self.dead:
            return
        deps = self.all_deps()
        for name, eng in self.engs.items():
            for d in deps:
                if d[0] != name:
                    eng.wait(d)

    def finish_all(self):
        eng = self.engs["sp"]
        for d in self.all_deps():
            if d[0] != "sp":
                eng.wait(d)


class Tl:
    def __init__(self, t, name, nb=1):
        self.t = t
        self.bs = [Buf("%s%d" % (name, i)) for i in range(nb)]
        self.b = self.bs[0]


class Ring:
    def __init__(self, tiles):
        self.tiles = tiles
        self.i = 0

    def get(self):
        t = self.tiles[self.i % len(self.tiles)]
        self.i += 1
        return t


class _Stop(Exception):
    pass


def build_nc(NL=4, stop=None, debug=False):
    def ck(name):
        f_ = fwbox[0]
        build_nc.marks.append((name, {k: e.count for k, e in f_.engs.items()}))
        if stop == name:
            f_.dead = True
    fwbox = [None]
    build_nc.marks = []
    nc = bass.Bass("TRN2", target_bir_lowering=False)

    def din(name, shape):
        return nc.dram_tensor(name, list(shape), F32, kind="ExternalInput").ap()

    def dout(name, shape):
        return nc.dram_tensor(name, list(shape), F32, kind="ExternalOutput").ap()

    x_prompt = din("x_prompt", [T, 1024])
    x_sample = din("x_sample", [TS, 1024])
    cache_a_k = din("cache_a_k", [2, T, 512])
    cache_a_v = din("cache_a_v", [2, T, 512])
    state_b = din("state_b", [2, 4, 64, 128])
    cache_c_k = din("cache_c_k", [2, 512, 1024])
    cache_c_v = din("cache_c_v", [2, 512, 1024])
    norm_mix_g = din("norm_mix_g", [4, 1024])
    norm_ffn_g = din("norm_ffn_g", [4, 1024])
    w_in_ab = din("w_in_ab", [2, 1024, 3088])
    w_gate_b = din("w_gate_b", [2, 16, 256])
    b_gate_b = din("b_gate_b", [2, 256])
    norm_gla_g = din("norm_gla_g", [2, 512])
    w_out_ab = din("w_out_ab", [2, 1024, 1024])
    w_qkv_c = din("w_qkv_c", [2, 1024, 3072])
    rel_bias_c = din("rel_bias_c", [2, 16, 192])
    w_out_c = din("w_out_c", [2, 1024, 1024])
    w_ffn_gate = din("w_ffn_gate", [4, 1024, DFF])
    w_ffn_up = din("w_ffn_up", [4, 1024, DFF])
    w_ffn_down = din("w_ffn_down", [4, DFF, 1024])
    norm_final_g = din("norm_final_g", [1024])

    y_prompt = dout("y_prompt", [T, 1024])
    y_sample = dout("y_sample", [TS, 1024])
    a_k_prompt = dout("a_k_prompt", [2, T, 512])
    a_v_prompt = dout("a_v_prompt", [2, T, 512])
    a_k_sample = dout("a_k_sample", [2, TS, 512])
    a_v_sample = dout("a_v_sample", [2, TS, 512])
    b_state_prompt = dout("b_state_prompt", [2, 4, 64, 128])
    b_state_sample = dout("b_state_sample", [2, 4, 64, 128])
    c_k_prompt = dout("c_k_prompt", [2, 512, 1024])
    c_v_prompt = dout("c_v_prompt", [2, 512, 1024])
    c_k_sample = dout("c_k_sample", [2, TS, 1024])
    c_v_sample = dout("c_v_sample", [2, TS, 1024])
    if debug:
        dbg_x = dout("dbg_x", [128, KC, TT])
        dbg_h = dout("dbg_h", [128, KC, TT])
        dbg_c = dout("dbg_c", [128, KC, TT])
    scr = nc.dram_tensor("scr_rel", [2, 16, 512], F32, kind="Internal").ap()
    scr_b = Buf("scr")

    with ExitStack() as st:
        fw = FW(nc, st)
        fwbox[0] = fw

        uniq = [0]

        def sb(name, shape, dt, nb=1, stack=None):
            uniq[0] += 1
            name = "%s_u%d" % (name, uniq[0])
            t = (stack or st).enter_context(nc.sbuf_tensor(name, list(shape), dt))
            return Tl(t, name, nb)

        def ring(name, shape, dt, n, stack=None):
            return Ring([sb("%s_%d" % (name, i), shape, dt, stack=stack) for i in range(n)])

        xT = sb("xT", [128, KC, TT], F32, nb=5)
        hT = sb("hT", [128, KC, TT], BF16, nb=5)
        catT = sb("catT", [128, KC, TT], BF16, nb=5)
        wring = ring("wb", [128, 2048], BF16, NW)
        psr = Ring([Tl(st.enter_context(nc.psum_tensor("ps%d" % i, [128, 512], F32)), "ps%d" % i)
                    for i in range(8)])
        for _t in psr.tiles:
            _t.b.excl = True
        ident = sb("ident", [128, 128], F32)
        ones_bf = sb("ones_bf", [128, 128], BF16)
        triS = sb("triS", [128, 128], BF16)
        ident_bf = sb("ident_bf", [128, 128], BF16)
        mneg = sb("mneg", [128, 128], BF16)
        triIncN = sb("triIncN", [128, 128], F32)
        triRevN = sb("triRevN", [128, 128], F32)
        mInc = sb("mInc", [128, 128], BF16)
        mask0 = sb("mask0", [128, 128], F32)
        maskA = sb("maskA", [128, 128], F32)
        antiJ = sb("antiJ", [128, 128], F32)
        gains = sb("gains", [128, 80], F32)
        sqr = ring("sq", [128, 512], BF16, 2)
        f32r = ring("f32t", [128, 512], F32, 2)
        rstd_r = ring("rstd", [128, 512], F32, 1)
        stg = ring("stg", [128, 1024], F32, 1)
        kvst = ring("kvst", [128, 128], F32, 2)

        def P(hold=False):
            while True:
                t_ = psr.get()
                if not getattr(t_, "held", False):
                    break
            t_.held = hold
            return t_

        def mm(out, lhsT, rhs, start, stop, R, W, **kw):
            fw.op("pe", lambda e: e.matmul(out, lhsT, rhs, start=start, stop=stop, **kw),
                  reads=R, writes=W, inc=True)

        def tr(out, in_, idn, R, W):
            fw.op("pe", lambda e: e.matmul(out, in_, idn, start=True, stop=True, is_transpose=True), reads=R, writes=W)

        def act(out, in_, func, R, W, **kw):
            fw.op("act", lambda e: e.activation(out, in_, func, **kw), reads=R, writes=W)

        def tt(out, in0, in1, op, R, W, eng="dve"):
            fw.op(eng, lambda e: e.tensor_tensor(out, in0, in1, op), reads=R, writes=W)

        def stt(out, in0, scalar, in1, op0, op1, R, W):
            fw.op("dve", lambda e: e.scalar_tensor_tensor(out, in0, scalar, in1, op0, op1), reads=R, writes=W)

        def tsc(out, in0, s1, op0, R, W, s2=None, op1=None, eng="dve"):
            if op1 is None:
                fw.op(eng, lambda e: e.tensor_scalar(out, in0, s1, None, op0), reads=R, writes=W)
            else:
                fw.op(eng, lambda e: e.tensor_scalar(out, in0, s1, s2, op0, op1), reads=R, writes=W)

        def cp(out, in_, R, W, eng="dve"):
            if eng == "act":
                fw.op("act", lambda e: e.copy(out, in_), reads=R, writes=W)
            else:
                fw.op(eng, lambda e: e.tensor_copy(out, in_), reads=R, writes=W)

        def ms(ap, val, W, eng="dve"):
            fw.op(eng, lambda e: e.memset(ap, val), writes=W)

        def asel(tl, pattern, cmp, base, cm):
            fw.op("pool", lambda e: e.affine_select(out=tl.t[:], in_=tl.t[:], pattern=pattern, compare_op=cmp,
                                                    fill=0.0, base=base, channel_multiplier=cm),
                  reads=[tl.b], writes=[tl.b])

        ms(ident.t[:], 1.0, [ident.b], "pool")
        asel(ident, [[-1, 128]], ALU.is_equal, 0, 1)
        ms(ones_bf.t[:], 1.0, [ones_bf.b], "pool")
        ms(triS.t[:], 1.0, [triS.b], "pool")
        asel(triS, [[-1, 128]], ALU.is_gt, 0, 1)
        ms(ident_bf.t[:], 1.0, [ident_bf.b], "pool")
        asel(ident_bf, [[-1, 128]], ALU.is_equal, 0, 1)
        ms(mneg.t[:], -192.0, [mneg.b], "pool")
        asel(mneg, [[-1, 128]], ALU.is_ge, 0, 1)
        ms(triIncN.t[:], -1.0 / 16.0, [triIncN.b], "pool")
        asel(triIncN, [[1, 128]], ALU.is_ge, 0, -1)
        ms(triIncN.t[0:64, 64:128], 0.0, [triIncN.b], "pool")
        ms(triRevN.t[:], -1.0 / 16.0, [triRevN.b], "pool")
        asel(triRevN, [[-1, 128]], ALU.is_gt, 0, 1)
        ms(triRevN.t[64:128, 0:64], 0.0, [triRevN.b], "pool")
        ms(mInc.t[:], 1.0, [mInc.b], "pool")
        asel(mInc, [[1, 128]], ALU.is_ge, 0, -1)
        ms(mInc.t[0:64, 64:128], 0.0, [mInc.b], "pool")
        ms(mask0.t[:], 1.0, [mask0.b], "pool")
        ms(mask0.t[64:128, 0:64], 0.0, [mask0.b], "pool")
        ms(maskA.t[:], 1.0, [maskA.b], "pool")
        ms(maskA.t[0:64, 64:128], 0.0, [maskA.b], "pool")
        ms(antiJ.t[:], 1.0, [antiJ.b], "pool")
        asel(antiJ, [[1, 128]], ALU.is_equal, -127, 1)

        g_st = stg.get()
        fw.dma("sp", g_st.t[0:32, 0:128], norm_mix_g.rearrange("l (k p) -> (l k) p", p=128), writes=[g_st.b])
        fw.dma("sp", g_st.t[32:64, 0:128], norm_ffn_g.rearrange("l (k p) -> (l k) p", p=128), writes=[g_st.b])
        fw.dma("sp", g_st.t[64:72, 0:128], norm_final_g.rearrange("(k p) -> k p", p=128), writes=[g_st.b])
        fw.dma("sp", g_st.t[72:80, 0:128], norm_gla_g.rearrange("l (k p) -> (l k) p", p=128), writes=[g_st.b])
        ps = P()
        tr(ps.t[:, 0:80], g_st.t[0:80, 0:128], ident.t[0:80, 0:80], [g_st.b, ident.b], [ps.b])
        cp(gains.t[:, :], ps.t[:, 0:80], [ps.b], [gains.b])

        for (lo, m) in TBLK:
            xs = stg.get()
            src = x_prompt[lo:lo + m, :] if lo < T else x_sample[:, :]
            fw.dma("sp", xs.t[0:m, :], src, writes=[xs.b])
            cb = xT.bs[chunk_of(lo)]
            for half in range(2):
                ps = P()
                for q in range(4):
                    kc = half * 4 + q
                    tr(ps.t[:, q * 128:q * 128 + m], xs.t[0:m, kc * 128:(kc + 1) * 128], ident.t[0:m, 0:m],
                       [xs.b, ident.b], [ps.b])
                pv = ps.t[:, :].rearrange("p (q c) -> p q c", q=4)[:, :, 0:m]
                cp(xT.t[:, half * 4:half * 4 + 4, lo:lo + m], pv, [ps.b], [cb], eng=("act" if half else "dve"))

        wsched = []

        def colw(W2d, c0, ncol):
            wsched.append(("col", W2d, c0, ncol))

        def roww(W2d, r0, nk):
            wsched.append(("row", W2d, r0, nk))

        ffn_groups = [(0, 8), (8, 8), (16, 6)]
        for L in range(NL):
            i = L // 2
            if L % 2 == 0:
                W = w_in_ab[i]
                for hp in range(4):
                    colw(W, 128 * hp, 128)
                    colw(W, 512 + 128 * hp, 128)
                    colw(W, 1024 + 128 * hp, 128)
                colw(W, 2560, 16)
                for hf in range(2):
                    colw(W, 1536 + 128 * hf, 128)
                    colw(W, 1792 + 128 * hf, 128)
                    colw(W, 2048 + 256 * hf, 256)
                for h in range(4):
                    colw(W, 2576 + 128 * h, 128)
                for r in range(4):
                    roww(w_out_ab[i], 256 * r, 2)
            else:
                W = w_qkv_c[i]
                for hp in range(8):
                    colw(W, 128 * hp, 128)
                    colw(W, 1024 + 128 * hp, 128)
                    colw(W, 2048 + 128 * hp, 128)
                for r in range(4):
                    roww(w_out_c[i], 256 * r, 2)
            for (j0, nj) in ffn_groups:
                for j in range(j0, j0 + nj):
                    colw(w_ffn_gate[L], 128 * j, 128)
                    colw(w_ffn_up[L], 128 * j, 128)
                for r in range(nj // 2):
                    roww(w_ffn_down[L], 128 * j0 + 256 * r, 2)

        wstate = {"issued": 0, "used": 0, "tiles": {}, "live": []}

        def w_issue():
            k = wstate["issued"]
            kind, W2d, a, b = wsched[k]
            tl = wring.get()
            if kind == "col":
                dst = tl.t[:, 0:KC * b].rearrange("p (k c) -> p k c", k=KC)
                src = W2d.rearrange("(k p) c -> p k c", p=128)[:, :, a:a + b]
            else:
                dst = tl.t[:, 0:b * 1024].rearrange("p (k c) -> p k c", k=b)
                src = W2d[a:a + 128 * b, :].rearrange("(k p) c -> p k c", p=128)
            fw.dma("pool", dst, src, writes=[tl.b])
            wstate["tiles"][k] = (tl, dst)
            wstate["issued"] = k + 1

        def wget(kind, a, b):
            k = wstate["used"]
            assert wsched[k][0] == kind and wsched[k][2] == a and wsched[k][3] == b, (k, wsched[k], kind, a, b)
            oldest = wstate["live"][0] if wstate["live"] else k
            assert k < oldest + NW
            while wstate["issued"] < min(len(wsched), oldest + NW):
                w_issue()
            wstate["used"] = k + 1
            wstate["live"].append(k)
            tl, view = wstate["tiles"].pop(k)
            return tl, view

        def wfree():
            wstate["live"] = []

        def norm(gcol0, dst, dst_is_h=True):
            for ci, (lo, n) in enumerate(TCH):
                rs = norm_stats(ci)
                for kc in range(KC):
                    stt(dst.t[:, kc, lo:lo + n], xT.t[:, kc, lo:lo + n], gains.t[:, gcol0 + kc:gcol0 + kc + 1],
                        rs.t[:, 0:n], ALU.mult, ALU.mult, [xT.bs[ci], rs.b, gains.b], [dst.bs[ci]])

        def norm_stats(ci, sq_ring=None, rs_ring=None):
            sq_ring = sq_ring or sqr
            rs_ring = rs_ring or rstd_r
            lo, n = TCH[ci]
            ps = P()
            for kc in range(KC):
                sq = sq_ring.get()
                act(sq.t[:, 0:n], xT.t[:, kc, lo:lo + n], AF.Square, [xT.bs[ci]], [sq.b])
                mm(ps.t[:, 0:n], ones_bf.t[:, :], sq.t[:, 0:n], kc == 0, kc == KC - 1, [ones_bf.b, sq.b], [ps.b])
            rs = rs_ring.get()
            act(rs.t[:, 0:n], ps.t[:, 0:n], AF.Ln, [ps.b], [rs.b], scale=1.0 / 1024.0, bias=EPS)
            act(rs.t[:, 0:n], rs.t[:, 0:n], AF.Exp, [rs.b], [rs.b], scale=-0.5)
            return rs

        def proj_fm(wv, wb, ncol, evac):
            for ci, (lo, n) in enumerate(TCH):
                ps = P()
                for kc in range(KC):
                    mm(ps.t[0:ncol, 0:n], wv[:, kc, :], hT.t[:, kc, lo:lo + n], kc == 0, kc == KC - 1,
                       [wb, hT.bs[ci]], [ps.b])
                evac(ps, ci, lo, n)

        def proj_tm(wv, wb, ncol, blocks, evac):
            for bi in blocks:
                lo, m = TBLK[bi]
                ps = P()
                for kc in range(KC):
                    mm(ps.t[0:m, 0:ncol], hT.t[:, kc, lo:lo + m], wv[:, kc, :], kc == 0, kc == KC - 1,
                       [wb, hT.bs[chunk_of(lo)]], [ps.b])
                evac(ps, bi, lo, m)

        def out_proj(nrows_tiles, src):
            wts = [wget("row", 256 * r, 2) for r in range(4)]
            for f in range(KC):
                for ci, (lo, n) in enumerate(TCH):
                    ps = P()
                    for k in range(KC):
                        tl, wv = wts[k // 2]
                        mm(ps.t[:, 0:n], wv[:, k % 2, 128 * f:128 * f + 128], src.t[:, k, lo:lo + n],
                           k == 0, k == KC - 1, [tl.b, src.bs[ci]], [ps.b])
                    tt(xT.t[:, f, lo:lo + n], xT.t[:, f, lo:lo + n], ps.t[:, 0:n], ALU.add,
                       [xT.bs[ci], ps.b], [xT.bs[ci]])
            wfree()

        def ffn(L):
            norm(32 + 8 * L, hT)
            for (j0, nj) in ffn_groups:
                for jj in range(nj):
                    j = j0 + jj
                    gtl, gv = wget("col", 128 * j, 128)
                    utl, uv = wget("col", 128 * j, 128)
                    for ci, (lo, n) in enumerate(TCH):
                        pg = P()
                        pu = P()
                        for kc in range(KC):
                            mm(pg.t[:, 0:n], gv[:, kc, :], hT.t[:, kc, lo:lo + n], kc == 0, kc == KC - 1,
                               [gtl.b, hT.bs[ci]], [pg.b])
                        for kc in range(KC):
                            mm(pu.t[:, 0:n], uv[:, kc, :], hT.t[:, kc, lo:lo + n], kc == 0, kc == KC - 1,
                               [utl.b, hT.bs[ci]], [pu.b])
                        sg = f32r.get()
                        act(sg.t[:, 0:n], pg.t[:, 0:n], AF.Silu, [pg.b], [sg.b])
                        tt(catT.t[:, jj, lo:lo + n], sg.t[:, 0:n], pu.t[:, 0:n], ALU.mult, [sg.b, pu.b], [catT.bs[ci]])
                    wfree()
                wts = [wget("row", 128 * j0 + 256 * r, 2) for r in range(nj // 2)]
                for f in range(KC):
                    for ci, (lo, n) in enumerate(TCH):
                        ps = P()
                        for k in range(nj):
                            tl, wv = wts[k // 2]
                            mm(ps.t[:, 0:n], wv[:, k % 2, 128 * f:128 * f + 128], catT.t[:, k, lo:lo + n],
                               k == 0, k == nj - 1, [tl.b, catT.bs[ci]], [ps.b])
                        tt(xT.t[:, f, lo:lo + n], xT.t[:, f, lo:lo + n], ps.t[:, 0:n], ALU.add,
                           [xT.bs[ci], ps.b], [xT.bs[ci]])
                wfree()

        def sb_items(W, po, q_ap, qb, nq, blocks, out_ap, out_b):
            stc = {}
            nb = len(blocks)
            items = []
            for bi_, blk_ in enumerate(blocks):
                class It:
                    pass
                it = It()
                it.bi, it.blk = bi_, blk_
                it.next_qoff = blocks[bi_ + 1]["qoff"] if bi_ + 1 < nb else None

                def z(it=it):
                    blk, bi = it.blk, it.bi
                    if bi == 0:
                        stc["pv"] = P(hold=True)
                        stc["cacc"] = P(hold=True)
                        ms(stc["cacc"].t[0:1, 0:nq], 0.0, [stc["cacc"].b])
                    it.pv, it.cacc = stc["pv"], stc["cacc"]
                    nk, qoff, diag = blk["nk"], blk["qoff"], blk["diag"]
                    n = nq - qoff
                    it.zps = P()
                    mm(it.zps.t[0:nk, 0:n], blk["kT"], q_ap[:, qoff:qoff + n], True, True, blk["R"] + [qb], [it.zps.b])
                    if diag:
                        mm(it.zps.t[0:nk, 0:nk], ident_bf.t[0:nk, 0:nk], mneg.t[0:nk, 0:nk], False, True,
                           [ident_bf.b, mneg.b], [it.zps.b], skip_group_check=True)

                def front_act(it=it):
                    blk = it.blk
                    nk, qoff = blk["nk"], blk["qoff"]
                    n = nq - qoff
                    it.sp = W["sp"].get()
                    act(it.sp.t[0:nk, 0:n], it.zps.t[0:nk, 0:n], AF.Exp, [it.zps.b], [it.sp.b], scale=-0.125)
                    act(it.sp.t[0:nk, 0:n], it.sp.t[0:nk, 0:n], AF.Ln, [it.sp.b], [it.sp.b], bias=1.0)

                def front_dve(it=it):
                    blk, bi = it.blk, it.bi
                    nk, qoff = blk["nk"], blk["qoff"]
                    n = nq - qoff
                    it.Lt = W["L"].get()
                    stt(it.Lt.t[0:nk, 0:n], it.zps.t[0:nk, 0:n], -0.125, it.sp.t[0:nk, 0:n], ALU.mult, ALU.subtract,
                        [it.zps.b, it.sp.b], [it.Lt.b])

                def colsum(it=it):
                    blk, bi = it.blk, it.bi
                    nk, qoff = blk["nk"], blk["qoff"]
                    n = nq - qoff
                    if bi < nb - 1:
                        cacc = it.cacc
                        mm(cacc.t[0:1, qoff:qoff + n], ones_bf.t[0:nk, 0:1], it.Lt.t[0:nk, 0:n], False, True,
                           [ones_bf.b, it.Lt.b], [cacc.b], skip_group_check=True)

                def snap(it=it):
                    bi = it.bi
                    if bi < nb - 1:
                        cacc = it.cacc
                        q2 = it.next_qoff
                        cs = W["cset"].get()
                        it.cnext = cs
                        cp(cs.t[0:1, q2:nq], cacc.t[0:1, q2:nq], [cacc.b], [cs.b])

                def tri(it=it):
                    blk, bi = it.blk, it.bi
                    nk, qoff = blk["nk"], blk["qoff"]
                    n = nq - qoff
                    it.aps = P()
                    first = (bi == 0)
                    mm(it.aps.t[0:nk, 0:n], triS.t[0:nk, 0:nk], it.Lt.t[0:nk, 0:n], True, first, [triS.b, it.Lt.b], [it.aps.b])
                    if not first:
                        chi = items[bi - 1].cnext
                        mm(it.aps.t[0:nk, 0:n], ones_bf.t[0:1, 0:nk], chi.t[0:1, qoff:qoff + n], False, True,
                           [ones_bf.b, chi.b], [it.aps.b])

                def tmp(it=it):
                    blk = it.blk
                    nk, qoff = blk["nk"], blk["qoff"]
                    n = nq - qoff
                    tt(it.sp.t[0:nk, 0:n], it.aps.t[0:nk, 0:n], it.sp.t[0:nk, 0:n], ALU.subtract, [it.aps.b, it.sp.b], [it.sp.b])

                def expw(it=it):
                    blk = it.blk
                    nk, qoff = blk["nk"], blk["qoff"]
                    n = nq - qoff
                    it.wt = W["w"].get()
                    act(it.wt.t[0:nk, 0:n], it.sp.t[0:nk, 0:n], AF.Exp, [it.sp.b], [it.wt.b])

                def pvm(it=it):
                    blk, bi = it.blk, it.bi
                    nk, qoff = blk["nk"], blk["qoff"]
                    n = nq - qoff
                    pv = it.pv
                    mm(pv.t[:, qoff:qoff + n], blk["v"], it.wt.t[0:nk, 0:n], bi == 0, bi == nb - 1,
                       blk["VR"] + [it.wt.b], [pv.b], skip_group_check=True)
                    if bi == nb - 1:
                        cp(out_ap, pv.t[po:po + 64, 0:nq], [pv.b], [out_b], eng="act")
                        pv.held = False
                        it.cacc.held = False

                it.z, it.front_act, it.front_dve, it.tri, it.tmp, it.expw, it.pvm = z, front_act, front_dve, tri, tmp, expw, pvm
                it.colsum, it.snap = colsum, snap
                items.append(it)
            return items

        def sb_run(items, W):
            n = len(items)
            G = lambda j: items[j] if 0 <= j < n else None
            for i in range(-3, n + 1):
                a, b_, c, d = G(i + 3), G(i + 1), G(i), G(i - 1)
                if a is not None:
                    a.z()
                if b_ is not None:
                    b_.colsum()
                if c is not None:
                    c.tri()
                if d is not None:
                    d.pvm()
                for _f in range(SB_FILL):
                    fps = W["fill"]
                    fw.op("pe", lambda e: e.matmul(fps.t[:, 0:512], hT.t[:, 1, 0:128], hT.t[:, 0, 0:512], start=True, stop=True),
                          reads=[], writes=[], inc=False)
                if b_ is not None:
                    b_.snap()
                if c is not None:
                    c.tmp()
                if a is not None:
                    a.front_act()
                if c is not None:
                    c.expw()
                if a is not None:
                    a.front_dve()

        def layer_ab(i, L):
            norm(8 * L, hT)
            ck("norm")
            with ExitStack() as ph:
                qT2 = sb("qT2", [128, TT], BF16, stack=ph)
                kT2 = sb("kT2", [128, TT + T], BF16, stack=ph)
                v2 = sb("v2", [128, 33, 128], BF16, stack=ph)
                kst = ring("kst", [128, 2, 128], F32, 1, stack=ph)
                W = {"cset": ring("chi", [1, 512], BF16, 3, stack=ph),
                     "sp": ring("spr", [128, 512], F32, 4, stack=ph),
                     "L": ring("Lt", [128, 512], BF16, 4, stack=ph), "w": ring("wt", [128, 512], BF16, 3, stack=ph)}
                for hp in range(4):
                    qtl, qv = wget("col", 128 * hp, 128)
                    proj_fm(qv, qtl.b, 128, lambda ps, ci, lo, n: cp(qT2.t[:, lo:lo + n], ps.t[:, 0:n], [ps.b], [qT2.b], eng="act"))
                    ktl, kv = wget("col", 512 + 128 * hp, 128)
                    proj_fm(kv, ktl.b, 128, lambda ps, ci, lo, n: cp(kT2.t[:, lo:lo + n], ps.t[:, 0:n], [ps.b], [kT2.b], eng="act"))

                    def k_evac(ps, bi, lo, m):
                        s = kvst.get()
                        cp(s.t[0:m, 0:128], ps.t[0:m, 0:128], [ps.b], [s.b], eng="act")
                        dst = a_k_prompt[i, lo:lo + m, 128 * hp:128 * hp + 128] if lo < T else a_k_sample[i, :, 128 * hp:128 * hp + 128]
                        fw.dma("sp", dst, s.t[0:m, 0:128], reads=[s.b])
                    proj_tm(kv, ktl.b, 128, range(17), k_evac)
                    vtl, vv = wget("col", 1024 + 128 * hp, 128)

                    def v_evac(ps, bi, lo, m):
                        s = kvst.get()
                        cp(s.t[0:m, 0:128], ps.t[0:m, 0:128], [ps.b], [s.b], eng="act")
                        cp(v2.t[0:m, bi, :], ps.t[0:m, 0:128], [ps.b], [v2.b])
                        dst = a_v_prompt[i, lo:lo + m, 128 * hp:128 * hp + 128] if lo < T else a_v_sample[i, :, 128 * hp:128 * hp + 128]
                        fw.dma("sp", dst, s.t[0:m, 0:128], reads=[s.b])
                    proj_tm(vv, vtl.b, 128, range(17), v_evac)
                    wfree()
                    fw.dma("pool", v2.t[:, 17:33, :],
                           cache_a_v[i].rearrange("(b p) c -> p b c", p=128)[:, :, 128 * hp:128 * hp + 128],
                           writes=[v2.b])
                    for g in range(4):
                        ps = P()
                        for g2 in range(2):
                            ks = kst.get()
                            r0 = 512 * g + 256 * g2
                            fw.dma("sp", ks.t[:, :, :],
                                   cache_a_k[i, r0:r0 + 256, :].rearrange("(b p) c -> p b c", p=128)[:, :, 128 * hp:128 * hp + 128],
                                   writes=[ks.b])
                            for q in range(2):
                                qq = 2 * g2 + q
                                tr(ps.t[:, qq * 128:(qq + 1) * 128], ks.t[:, q, :], ident.t[:, :], [ks.b, ident.b], [ps.b])
                        c0 = TT + 512 * g
                        cp(kT2.t[:, c0:c0 + 512], ps.t[:, :], [ps.b], [kT2.b], eng="act")
                    ck("sbproj")
                    items = []
                    for hh in range(2):
                        po = 64 * hh
                        for c in range(4):
                            lo = 512 * c
                            blocks = []
                            for b in range(4 * c + 3, -1, -1):
                                blocks.append(dict(kT=kT2.t[po:po + 64, 128 * b:128 * b + 128], v=v2.t[:, b, :], nk=128,
                                                   qoff=max(0, 128 * b - lo), diag=(b >= 4 * c), R=[kT2.b], VR=[v2.b]))
                            items += sb_items(W, po, qT2.t[po:po + 64, lo:lo + 512], qT2.b, 512, blocks,
                                              catT.t[po:po + 64, hp, lo:lo + 512], catT.bs[c])
                        blocks = [dict(kT=kT2.t[po:po + 64, T:TT], v=v2.t[0:16, 16, :], nk=16, qoff=0, diag=True, R=[kT2.b], VR=[v2.b])]
                        for b in range(15, -1, -1):
                            blocks.append(dict(kT=kT2.t[po:po + 64, TT + 128 * b:TT + 128 * b + 128], v=v2.t[:, 17 + b, :], nk=128,
                                               qoff=0, diag=False, R=[kT2.b], VR=[v2.b]))
                        items += sb_items(W, po, qT2.t[po:po + 64, T:TT], qT2.b, 16, blocks,
                                          catT.t[po:po + 64, hp, T:TT], catT.bs[4])
                    W["fill"] = P(hold=True) if SB_FILL else None
                    sb_run(items, W)
                    if SB_FILL:
                        fw.op("pe", lambda e: e.matmul(W["fill"].t[:, 0:512], ones_bf.t[:, :], hT.t[:, 0, 0:512], start=True, stop=True),
                              reads=[], writes=[W["fill"].b])
                        W["fill"].held = False
                    ck("sbhp0")
                fw.barrier()
            ck("sb")
            gla(i, L)
            ck("gla")
            out_proj(4, catT)

        def gla(i, L):
            with ExitStack() as ph:
                glrT = sb("glrT", [16, TT], BF16, stack=ph)
                wg = sb("wg", [16, 256], BF16, stack=ph)
                bg = sb("bg", [1, 256], BF16, stack=ph)
                qg = sb("qg", [128, TT], BF16, stack=ph)
                kg = sb("kg", [128, TT], BF16, stack=ph)
                kd = sb("kd", [128, 17, 128], BF16, stack=ph)
                vt = sb("vt", [128, 17, 256], BF16, stack=ph)
                spg_r = ring("spg", [128, 128], F32, 2, stack=ph)
                ebl = sb("ebl", [128, 34], F32, stack=ph)
                Sf = sb("Sf", [128, 128], F32, stack=ph)
                Sb = sb("Sb", [128, 128], BF16, stack=ph)
                Sf2 = sb("Sf2", [128, 128], F32, stack=ph)
                Sb2 = sb("Sb2", [128, 128], BF16, stack=ph)
                att_r = ring("att", [128, 2, 128], BF16, 2, stack=ph)
                fw.dma("pool", wg.t[:, :], w_gate_b[i], writes=[wg.b])
                fw.dma("pool", bg.t[:, :], b_gate_b[i:i + 1, :], writes=[bg.b])
                gtl, gv = wget("col", 2560, 16)
                proj_fm(gv, gtl.b, 16, lambda ps, ci, lo, n: cp(glrT.t[0:16, lo:lo + n], ps.t[0:16, 0:n], [ps.b], [glrT.b], eng="act"))
                wfree()
                for hf in range(2):
                    qtl, qv = wget("col", 1536 + 128 * hf, 128)
                    ktl, kv = wget("col", 1792 + 128 * hf, 128)
                    vtl, vv = wget("col", 2048 + 256 * hf, 256)
                    for bi, (lo, m) in enumerate(TBLK):
                        hb = hT.bs[chunk_of(lo)]
                        ps = P()
                        mm(ps.t[0:m, 0:128], glrT.t[0:16, lo:lo + m], wg.t[0:16, 128 * hf:128 * hf + 128], True, False, [glrT.b, wg.b], [ps.b])
                        mm(ps.t[0:m, 0:128], ones_bf.t[0:1, 0:m], bg.t[0:1, 128 * hf:128 * hf + 128], False, True, [ones_bf.b, bg.b], [ps.b])
                        spg = spg_r.get()
                        act(spg.t[0:m, :], ps.t[0:m, 0:128], AF.Exp, [ps.b], [spg.b], scale=-1.0)
                        act(spg.t[0:m, :], spg.t[0:m, :], AF.Ln, [spg.b], [spg.b], bias=1.0)
                        pb = P()
                        mm(pb.t[:, 0:m], spg.t[0:m, :], triIncN.t[0:m, 0:m], True, True, [spg.b, triIncN.b], [pb.b])
                        eb = f32r.get()
                        act(eb.t[:, 0:m], pb.t[:, 0:m], AF.Exp, [pb.b], [eb.b])
                        act(eb.t[:, 128:128 + m], pb.t[:, 0:m], AF.Exp, [pb.b, eb.b], [eb.b], scale=-1.0)
                        if m == 128:
                            act(ebl.t[:, 2 * bi:2 * bi + 1], pb.t[:, 63:64], AF.Exp, [pb.b], [ebl.b])
                            act(ebl.t[:, 2 * bi + 1:2 * bi + 2], pb.t[:, 127:128], AF.Exp, [pb.b], [ebl.b])
                        else:
                            act(ebl.t[:, 32:33], pb.t[:, 15:16], AF.Exp, [pb.b], [ebl.b])
                        pq = P()
                        for kc in range(KC):
                            mm(pq.t[:, 0:m], qv[:, kc, :], hT.t[:, kc, lo:lo + m], kc == 0, kc == KC - 1, [qtl.b, hb], [pq.b])
                        stt(qg.t[:, lo:lo + m], pq.t[:, 0:m], 0.125, eb.t[:, 0:m], ALU.mult, ALU.mult, [pq.b, eb.b], [qg.b])
                        pk = P()
                        for kc in range(KC):
                            mm(pk.t[:, 0:m], kv[:, kc, :], hT.t[:, kc, lo:lo + m], kc == 0, kc == KC - 1, [ktl.b, hb], [pk.b])
                        tt(kg.t[:, lo:lo + m], pk.t[:, 0:m], eb.t[:, 128:128 + m], ALU.mult, [pk.b, eb.b], [kg.b])
                        pr = P()
                        mm(pr.t[0:m, 0:128], triRevN.t[0:m, 0:m], spg.t[0:m, :], True, True, [spg.b, triRevN.b], [pr.b])
                        er = f32r.get()
                        act(er.t[0:m, 0:128], pr.t[0:m, 0:128], AF.Exp, [pr.b], [er.b])
                        pkt = P()
                        for kc in range(KC):
                            mm(pkt.t[0:m, 0:128], hT.t[:, kc, lo:lo + m], kv[:, kc, :], kc == 0, kc == KC - 1, [ktl.b, hb], [pkt.b])
                        tt(kd.t[0:m, bi, :], pkt.t[0:m, 0:128], er.t[0:m, 0:128], ALU.mult, [pkt.b, er.b], [kd.b])
                        pvt = P()
                        for kc in range(KC):
                            mm(pvt.t[0:m, 0:256], hT.t[:, kc, lo:lo + m], vv[:, kc, :], kc == 0, kc == KC - 1, [vtl.b, hb], [pvt.b])
                        cp(vt.t[0:m, bi, :], pvt.t[0:m, 0:256], [pvt.b], [vt.b], eng="act")
                    wfree()
                    ck("glaproj")
                    ms(Sf.t[:, :], 0.0, [Sf.b])
                    ms(Sb.t[:, :], 0.0, [Sb.b])
                    for hh in range(2):
                        fw.dma("sp", Sf2.t[64 * hh:64 * hh + 64, :], state_b[i, 2 * hf + hh], writes=[Sf2.b])
                    cp(Sb2.t[:, :], Sf2.t[:, :], [Sf2.b], [Sb2.b])
                    ck("recA")
                    for bi, (lo, m) in enumerate(TBLK):
                        S_f, S_b = (Sf, Sb) if m == 128 else (Sf2, Sb2)
                        cb = catT.bs[chunk_of(lo)]
                        pas = [P(), P()]
                        for hh in range(2):
                            po = 64 * hh
                            mm(pas[hh].t[0:m, 0:m], kg.t[po:po + 64, lo:lo + m], qg.t[po:po + 64, lo:lo + m],
                               True, True, [kg.b, qg.b], [pas[hh].b])
                        at = att_r.get()
                        for hh in range(2):
                            tt(at.t[0:m, hh, 0:m], pas[hh].t[0:m, 0:m], mInc.t[0:m, 0:m], ALU.mult,
                               [pas[hh].b, mInc.b], [at.b])
                        ck("recB")
                        pos = [P(), P()]
                        nsub = 2 if m == 128 else 1
                        L_ = 64 if m == 128 else 16
                        for sub in range(nsub):
                            c0 = sub * L_
                            for hh in range(2):
                                po = 64 * hh
                                po_ps = pos[hh]
                                if sub == 0:
                                    mm(po_ps.t[:, 0:m], vt.t[0:m, bi, 128 * hh:128 * hh + 128], at.t[0:m, hh, 0:m],
                                       True, False, [vt.b, at.b], [po_ps.b], skip_group_check=True)
                                mm(po_ps.t[:, c0:c0 + L_], S_b.t[po:po + 64, :], qg.t[po:po + 64, lo + c0:lo + c0 + L_],
                                   False, (sub == nsub - 1), [S_b.b, qg.b], [po_ps.b], skip_group_check=True)
                            ck("recC")
                            pss = P()
                            for hh in range(2):
                                mm(pss.t[:, 128 * hh:128 * hh + 128], kd.t[c0:c0 + L_, bi, :], vt.t[c0:c0 + L_, bi, 128 * hh:128 * hh + 128],
                                   True, True, [kd.b, vt.b], [pss.b])
                            cidx = (2 * bi + sub) if m == 128 else 32
                            for hh in range(2):
                                po = 64 * hh
                                stt(S_f.t[po:po + 64, :], S_f.t[po:po + 64, :], ebl.t[po:po + 64, cidx:cidx + 1],
                                    pss.t[po:po + 64, 128 * hh:128 * hh + 128], ALU.mult, ALU.add, [S_f.b, ebl.b, pss.b], [S_f.b])
                            cp(S_b.t[:, :], S_f.t[:, :], [S_f.b], [S_b.b], eng="act")
                            ck("recD")
                        for hh in range(2):
                            cp(catT.t[:, 4 + 2 * hf + hh, lo:lo + m], pos[hh].t[:, 0:m], [pos[hh].b], [cb], eng="act")
                        ck("rec1")
                    for hh in range(2):
                        fw.dma("sp", b_state_prompt[i, 2 * hf + hh], Sf.t[64 * hh:64 * hh + 64, :], reads=[Sf.b])
                        fw.dma("sp", b_state_sample[i, 2 * hf + hh], Sf2.t[64 * hh:64 * hh + 64, :], reads=[Sf2.b])
                ck("rec")
                for h in range(4):
                    rtl, rv = wget("col", 2576 + 128 * h, 128)
                    for ci, (lo, n) in enumerate(TCH):
                        cb = catT.bs[ci]
                        sq = sqr.get()
                        act(sq.t[:, 0:n], catT.t[:, 4 + h, lo:lo + n], AF.Square, [cb], [sq.b])
                        pss = P()
                        mm(pss.t[:, 0:n], ones_bf.t[:, :], sq.t[:, 0:n], True, True, [ones_bf.b, sq.b], [pss.b])
                        rs = rstd_r.get()
                        act(rs.t[:, 0:n], pss.t[:, 0:n], AF.Ln, [pss.b], [rs.b], scale=1.0 / 128.0, bias=EPS)
                        act(rs.t[:, 0:n], rs.t[:, 0:n], AF.Exp, [rs.b], [rs.b], scale=-0.5)
                        pr = P()
                        for kc in range(KC):
                            mm(pr.t[:, 0:n], rv[:, kc, :], hT.t[:, kc, lo:lo + n], kc == 0, kc == KC - 1, [rtl.b, hT.bs[ci]], [pr.b])
                        er = f32r.get()
                        act(er.t[:, 0:n], pr.t[:, 0:n], AF.Exp, [pr.b], [er.b], scale=-1.0)
                        act(er.t[:, 0:n], er.t[:, 0:n], AF.Ln, [er.b], [er.b], bias=1.0)
                        act(er.t[:, 0:n], er.t[:, 0:n], AF.Exp, [er.b], [er.b], scale=-1.0)
                        tt(er.t[:, 0:n], er.t[:, 0:n], pr.t[:, 0:n], ALU.mult, [er.b, pr.b], [er.b])
                        stt(rs.t[:, 0:n], rs.t[:, 0:n], gains.t[:, 72 + 4 * i + h:73 + 4 * i + h], er.t[:, 0:n],
                            ALU.mult, ALU.mult, [rs.b, gains.b, er.b], [rs.b])
                        tt(catT.t[:, 4 + h, lo:lo + n], catT.t[:, 4 + h, lo:lo + n], rs.t[:, 0:n], ALU.mult, [cb, rs.b], [cb])
                    wfree()
                fw.barrier()

        def layer_c(i, L):
            norm(8 * L, hT)
            with ExitStack() as ph:
                qT2 = sb("cqT2", [128, TT], BF16, stack=ph)
                kT2 = sb("ckT2", [128, TT + 512], BF16, stack=ph)
                v2 = sb("cv2", [128, 21, 128], BF16, stack=ph)
                kst = sb("ckst", [128, 4, 128], F32, stack=ph)
                ext = sb("ext", [16, 512], F32, stack=ph)
                Gst = ring("Gst", [128, 3, 128], F32, 2, stack=ph)
                Mh = sb("Mh", [128, 2, 5, 128], BF16, stack=ph)
                Pr = ring("Pr", [128, 5, 128], BF16, 3, stack=ph)
                rden_r = ring("rden", [128, 128], F32, 2, stack=ph)
                fw.dma("sp", ext.t[:, 64:256], rel_bias_c[i], writes=[ext.b])
                cp(ext.t[:, 0:64], ext.t[:, 64:65].to_broadcast([16, 64]), [ext.b], [ext.b])
                cp(ext.t[:, 256:512], ext.t[:, 255:256].to_broadcast([16, 256]), [ext.b], [ext.b])
                act(ext.t[:, :], ext.t[:, :], AF.Exp, [ext.b], [ext.b])
                fw.dma("sp", scr[i], ext.t[:, :], reads=[ext.b], writes=[scr_b])
                ck("cext")
                for hp in range(8):
                    qtl, qv = wget("col", 128 * hp, 128)
                    proj_fm(qv, qtl.b, 128, lambda ps, ci, lo, n: cp(qT2.t[:, lo:lo + n], ps.t[:, 0:n], [ps.b], [qT2.b], eng="act"))
                    ktl, kv = wget("col", 1024 + 128 * hp, 128)
                    proj_fm(kv, ktl.b, 128, lambda ps, ci, lo, n: cp(kT2.t[:, lo:lo + n], ps.t[:, 0:n], [ps.b], [kT2.b], eng="act"))

                    def k_evac(ps, bi, lo, m):
                        s = kvst.get()
                        cp(s.t[0:m, 0:128], ps.t[0:m, 0:128], [ps.b], [s.b], eng="act")
                        dst = c_k_prompt[i, lo - 1536:lo - 1536 + m, 128 * hp:128 * hp + 128] if lo < T else c_k_sample[i, :, 128 * hp:128 * hp + 128]
                        fw.dma("sp", dst, s.t[0:m, 0:128], reads=[s.b])
                    ck("cqk")
                    proj_tm(kv, ktl.b, 128, range(12, 17), k_evac)
                    ck("ckev")
                    vtl, vv = wget("col", 2048 + 128 * hp, 128)

                    def v_evac(ps, bi, lo, m):
                        cp(v2.t[0:m, bi, :], ps.t[0:m, 0:128], [ps.b], [v2.b])
                        if bi >= 12:
                            s = kvst.get()
                            cp(s.t[0:m, 0:128], ps.t[0:m, 0:128], [ps.b], [s.b], eng="act")
                            dst = c_v_prompt[i, lo - 1536:lo - 1536 + m, 128 * hp:128 * hp + 128] if lo < T else c_v_sample[i, :, 128 * hp:128 * hp + 128]
                            fw.dma("sp", dst, s.t[0:m, 0:128], reads=[s.b])
                    proj_tm(vv, vtl.b, 128, range(17), v_evac)
                    ck("cv")
                    wfree()
                    fw.dma("pool", v2.t[:, 17:21, :],
                           cache_c_v[i].rearrange("(b p) c -> p b c", p=128)[:, :, 128 * hp:128 * hp + 128], writes=[v2.b])
                    ck("cvc")
                    fw.dma("sp", kst.t[:, :, :],
                           cache_c_k[i].rearrange("(b p) c -> p b c", p=128)[:, :, 128 * hp:128 * hp + 128], writes=[kst.b])
                    ps = P()
                    for q in range(4):
                        tr(ps.t[:, q * 128:(q + 1) * 128], kst.t[:, q, :], ident.t[:, :], [kst.b, ident.b], [ps.b])
                    cp(kT2.t[:, TT:TT + 512], ps.t[:, :], [ps.b], [kT2.b], eng="act")
                    ck("ccache")
                    for hh in range(2):
                        h = 2 * hp + hh
                        G = Gst.get()
                        base = (i * 16 + h) * 512
                        fw.dma("sp", G.t[:, 0, :], bass.AP(tensor=scr.tensor, offset=base + 0, ap=[[1, 128], [1, 128]]), reads=[scr_b], writes=[G.b])
                        fw.dma("sp", G.t[:, 1, :], bass.AP(tensor=scr.tensor, offset=base + 128, ap=[[1, 128], [1, 128]]), reads=[scr_b], writes=[G.b])
                        fw.dma("sp", G.t[:, 2, :], bass.AP(tensor=scr.tensor, offset=base + 300, ap=[[0, 128], [1, 128]]), reads=[scr_b], writes=[G.b])
                        pe0 = P()
                        mm(pe0.t[:, 0:128], antiJ.t[:, :], G.t[:, 0, :], True, True, [antiJ.b, G.b], [pe0.b])
                        mm(pe0.t[:, 128:256], antiJ.t[:, :], G.t[:, 1, :], True, True, [antiJ.b, G.b], [pe0.b])
                        tt(Mh.t[:, hh, 4, :], pe0.t[:, 0:128], mask0.t[:, :], ALU.mult, [pe0.b, mask0.b], [Mh.b])
                        cp(Mh.t[:, hh, 3, :], pe0.t[:, 128:256], [pe0.b], [Mh.b])
                        cp(Mh.t[:, hh, 2, :], G.t[:, 2, :], [G.b], [Mh.b])
                        cp(Mh.t[:, hh, 1, :], G.t[:, 2, :], [G.b], [Mh.b])
                        tt(Mh.t[:, hh, 0, :], G.t[:, 2, :], maskA.t[:, :], ALU.mult, [G.b, maskA.b], [Mh.b])
                    ck("cproj")
                    citems = []
                    for hh in range(2):
                        po = 64 * hh
                        for bi, (lo, m) in enumerate(TBLK):
                            if m == 128:
                                srcs = []
                                for s_ in range(5):
                                    kb = bi - 4 + s_
                                    if kb >= 0:
                                        srcs.append((s_, kT2.t[po:po + 64, 128 * kb:128 * kb + 128], v2.t[:, kb, :], 128))
                            else:
                                srcs = [(s_, kT2.t[po:po + 64, TT + 128 * s_:TT + 128 * s_ + 128], v2.t[:, 17 + s_, :], 128) for s_ in range(4)]
                                srcs.append((4, kT2.t[po:po + 64, T:TT], v2.t[0:16, 16, :], 16))

                            class CIt:
                                pass
                            it = CIt()

                            def cz(it=it, srcs=srcs, po=po, lo=lo, m=m):
                                it.pz = P()
                                it.pz2 = P()
                                for (s_, kap, vap, nk) in srcs:
                                    tgt = it.pz.t[0:nk, 128 * s_:128 * s_ + m] if s_ < 4 else it.pz2.t[0:nk, 0:m]
                                    mm(tgt, kap, qT2.t[po:po + 64, lo:lo + m], True, True, [kT2.b, qT2.b], [it.pz.b if s_ < 4 else it.pz2.b])

                            def cexp(it=it, srcs=srcs, m=m):
                                pz, pz2 = it.pz, it.pz2
                                s0 = srcs[0][0]
                                it.Pt = Pt = Pr.get()
                                if s0 < 4:
                                    if m == 128:
                                        act(Pt.t[:, s0:4, :], pz.t[:, 128 * s0:512].rearrange("p (s c) -> p s c", c=128), AF.Exp,
                                            [pz.b], [Pt.b], scale=0.125)
                                    else:
                                        act(Pt.t[:, 0:4, 0:m], pz.t[:, :].rearrange("p (s c) -> p s c", c=128)[:, :, 0:m], AF.Exp,
                                            [pz.b], [Pt.b], scale=0.125)
                                nk4 = srcs[-1][3]
                                act(Pt.t[0:nk4, 4, 0:m], pz2.t[0:nk4, 0:m], AF.Exp, [pz2.b], [Pt.b], scale=0.125)

                            def cmult(it=it, srcs=srcs, m=m, hh=hh):
                                Pt = it.Pt
                                s0 = srcs[0][0]
                                if m == 128:
                                    tt(Pt.t[:, s0:5, :], Pt.t[:, s0:5, :], Mh.t[:, hh, s0:5, :], ALU.mult, [Pt.b, Mh.b], [Pt.b])
                                else:
                                    tt(Pt.t[:, 0:4, 0:m], Pt.t[:, 0:4, 0:m], Mh.t[:, hh, 0:4, 0:m], ALU.mult, [Pt.b, Mh.b], [Pt.b])
                                    tt(Pt.t[0:16, 4, 0:m], Pt.t[0:16, 4, 0:m], Mh.t[0:16, hh, 4, 0:m], ALU.mult, [Pt.b, Mh.b], [Pt.b])

                            def cpv(it=it, srcs=srcs, m=m):
                                Pt = it.Pt
                                it.pvv = pvv = P()
                                it.pdn = pdn = P()
                                for k_, (s_, kap, vap, nk) in enumerate(srcs):
                                    mm(pvv.t[:, 0:m], vap, Pt.t[0:nk, s_, 0:m], k_ == 0, k_ == len(srcs) - 1, [v2.b, Pt.b], [pvv.b])
                                for k_, (s_, kap, vap, nk) in enumerate(srcs):
                                    mm(pdn.t[:, 0:m], ones_bf.t[0:nk, :], Pt.t[0:nk, s_, 0:m], k_ == 0, k_ == len(srcs) - 1, [ones_bf.b, Pt.b], [pdn.b])

                            def cnorm_a(it=it, po=po, lo=lo, m=m, hp=hp):
                                pdn = it.pdn
                                it.rd = rd = rden_r.get()
                                act(rd.t[po:po + 64, 0:m], pdn.t[po:po + 64, 0:m], AF.Ln, [pdn.b], [rd.b])
                                act(rd.t[po:po + 64, 0:m], rd.t[po:po + 64, 0:m], AF.Exp, [rd.b], [rd.b], scale=-1.0)

                            def cnorm(it=it, po=po, lo=lo, m=m, hp=hp):
                                pvv, rd = it.pvv, it.rd
                                tt(catT.t[po:po + 64, hp, lo:lo + m], pvv.t[po:po + 64, 0:m], rd.t[po:po + 64, 0:m], ALU.mult,
                                   [pvv.b, rd.b], [catT.bs[chunk_of(lo)]])

                            it.cnorm_a = cnorm_a
                            it.cz, it.cexp, it.cmult, it.cpv, it.cnorm = cz, cexp, cmult, cpv, cnorm
                            citems.append(it)
                    nci = len(citems)
                    citems[0].cz()
                    citems[0].cexp()
                    for ii in range(nci):
                        if ii + 1 < nci:
                            citems[ii + 1].cz()
                        citems[ii].cmult()
                        if ii >= 1:
                            citems[ii - 1].cnorm_a()
                        if ii + 1 < nci:
                            citems[ii + 1].cexp()
                        citems[ii].cpv()
                        if ii >= 1:
                            citems[ii - 1].cnorm()
                    citems[nci - 1].cnorm_a()
                    citems[nci - 1].cnorm()
                    ck("chp0")
                fw.barrier()
            out_proj(4, catT)

        ck("xload")
        for L in range(NL):
            if L % 2 == 0:
                layer_ab(L // 2, L)
            else:
                layer_c(L // 2, L)
            ck("mix%d" % L)
            ffn(L)
            ck("ffn%d" % L)

        if debug:
            fw.dead = False
            fw.barrier()
            fw.dma("sp", dbg_x, xT.t[:, :, :], reads=xT.bs)
            fw.dma("pool", dbg_h, hT.t[:, :, :], reads=hT.bs)
            fw.dma("pool", dbg_c, catT.t[:, :, :], reads=catT.bs)
            if stop is not None:
                fw.dead = True
        yt_r = ring("ytmp", [128, 8, 128], F32, 2)
        for ci, (lo, n) in enumerate(TCH):
            rs = norm_stats(ci)
            for b0 in range(0, n, 128):
                m = min(128, n - b0)
                yt = yt_r.get()
                for kc in range(KC):
                    stt(yt.t[:, kc, 0:m], xT.t[:, kc, lo + b0:lo + b0 + m], gains.t[:, 64 + kc:65 + kc], rs.t[:, b0:b0 + m],
                        ALU.mult, ALU.mult, [xT.bs[ci], rs.b, gains.b], [yt.b])
                ys = stg.get()
                for half in range(2):
                    ps = P()
                    for q in range(4):
                        tr(ps.t[0:m, q * 128:(q + 1) * 128], yt.t[:, half * 4 + q, 0:m], ident.t[:, :], [yt.b, ident.b], [ps.b])
                    cp(ys.t[0:m, half * 512:half * 512 + 512], ps.t[0:m, :], [ps.b], [ys.b], eng=("act" if half else "dve"))
                dst = y_prompt[lo + b0:lo + b0 + m, :] if lo < T else y_sample[:, :]
                fw.dma("sp", dst, ys.t[0:m, :], reads=[ys.b])
        fw.dead = False
        fw.finish_all()
        build_nc.stats = (fw.ninstr, fw.nwaits)
    return nc


IN_NAMES = ["x_prompt", "x_sample", "cache_a_k", "cache_a_v", "state_b", "cache_c_k", "cache_c_v",
            "norm_mix_g", "norm_ffn_g", "w_in_ab", "w_gate_b", "b_gate_b", "norm_gla_g", "w_out_ab",
            "w_qkv_c", "rel_bias_c", "w_out_c", "w_ffn_gate", "w_ffn_up", "w_ffn_down", "norm_final_g"]


def make_in_maps(inputs, n=8):
    f = lambda a: np.ascontiguousarray(np.asarray(a, dtype=np.float32))
    maps = []
    shared = {k: f(inputs[k]) for k in IN_NAMES[7:]}
    for b in range(n):
        m = dict(shared)
        m["x_prompt"] = f(inputs["x_prompt"][b])
        m["x_sample"] = f(inputs["x_sample"][b])
        m["cache_a_k"] = f(np.asarray(inputs["cache_a_k"])[:, b].reshape(2, T, 512))
        m["cache_a_v"] = f(np.asarray(inputs["cache_a_v"])[:, b].reshape(2, T, 512))
        m["state_b"] = f(np.asarray(inputs["state_b"])[:, b])
        m["cache_c_k"] = f(np.asarray(inputs["cache_c_k"])[:, b].reshape(2, 512, 1024))
        m["cache_c_v"] = f(np.asarray(inputs["cache_c_v"])[:, b].reshape(2, 512, 1024))
        maps.append(m)
    return maps


def gather(results):
    n = len(results)
    st = lambda k: np.stack([np.asarray(r[k], dtype=np.float32) for r in results], axis=0)
    y_prompt = st("y_prompt")
    y_sample = st("y_sample")
    ab = lambda k, tt_: np.transpose(st(k), (1, 0, 2, 3)).reshape(2, n, tt_, 8, 64)
    cc = lambda k, tt_: np.transpose(st(k), (1, 0, 2, 3)).reshape(2, n, tt_, 16, 64)
    bs = lambda k: np.transpose(st(k), (1, 0, 2, 3, 4))
    return (y_prompt, y_sample, ab("a_k_prompt", T), ab("a_v_prompt", T), ab("a_k_sample", TS), ab("a_v_sample", TS),
            bs("b_state_prompt"), bs("b_state_sample"), cc("c_k_prompt", 512), cc("c_v_prompt", 512),
            cc("c_k_sample", TS), cc("c_v_sample", TS))


def kernel(**inputs):
    nc = build_nc(4)
    in_maps = make_in_maps(inputs, 8)
    res = run_bass_kernel_spmd(nc, in_maps, core_ids=list(range(8)))
    return gather(res.results)
```
